# Optimizing a Trainium2 kernel written in Bass

```python
import math
import jax
import jax.numpy as jnp
from jax import lax
import numpy as np

D_MODEL = 2048
BATCH = 2
SEQ = 4096
DEPTH = 4

CTX_LEN = 256
GRID_W = 64
N_MOD = 6
EPS = 1e-6
N_BRANCH = 3
BRANCH_WIDTH = D_MODEL // 2
RG_WIDTH = BRANCH_WIDTH
RG_BLOCKS = 8
RG_BLOCK_DIM = RG_WIDTH // RG_BLOCKS
RG_CONV_W = 4
RG_CONV_LEFT = 2
RG_C = 8.0
DA_HEADS = 8
DA_HEAD_DIM = BRANCH_WIDTH // (2 * DA_HEADS)
DA_WIDTH = DA_HEADS * 2 * DA_HEAD_DIM
ROPE_AXIS_DIM = DA_HEAD_DIM // 2
ROPE_BASE = 10000.0
Q_BLOCK = 128
HG_HEADS = 8
HG_DK = BRANCH_WIDTH // HG_HEADS
HG_DV = BRANCH_WIDTH // HG_HEADS
HG_WIDTH = HG_HEADS * HG_DV
HG_CHUNK = 64
LOGF_FLOOR = 1e-20
PEER_HEADS = 8
PEER_NKEYS = 128
PEER_N = PEER_NKEYS * PEER_NKEYS
PEER_DQ = 256
PEER_TOPK = 16
PEER_TOK_BLOCK = 128

IN_NAMES = ('rg_x', 'rg_gate', 'da_q', 'da_k', 'da_v', 'hg_q', 'hg_f_fwd', 'hg_f_bwd', 'hg_i', 'hg_g', 'gates')
IN_WIDTHS = (RG_WIDTH, RG_WIDTH, 2 * DA_HEADS * DA_HEAD_DIM, 2 * DA_HEADS * DA_HEAD_DIM, DA_WIDTH,
             HG_HEADS * HG_DK, HG_HEADS * HG_DK, HG_HEADS * HG_DK, HG_WIDTH, HG_WIDTH, N_BRANCH * D_MODEL)
IN_WIDTH = sum(IN_WIDTHS)
IN_OFFSETS = tuple(int(v) for v in np.cumsum(IN_WIDTHS)[:-1])

kernel_name = 'hybrid_diffusion_rglru_diffattn_hgrn2_peer'


def rms_norm(x, g):
    xf = x.astype(jnp.float32)
    y = xf * lax.rsqrt(jnp.mean(xf * xf, axis=-1, keepdims=True) + EPS)
    return (y * g).astype(x.dtype)


def modulate(h, shift, scale):
    return h * (1 + scale) + shift


def axial_rope_tables(rows):
    r = jnp.repeat(jnp.arange(rows, dtype=jnp.float32), GRID_W)
    col = jnp.tile(jnp.arange(GRID_W, dtype=jnp.float32), rows)
    inv = ROPE_BASE ** (-jnp.arange(0, ROPE_AXIS_DIM, 2, dtype=jnp.float32) / ROPE_AXIS_DIM)
    ang = jnp.concatenate([r[:, None] * inv, col[:, None] * inv], axis=-1)
    return jnp.cos(ang), jnp.sin(ang)


def apply_rope(x, cos, sin):
    half = x.shape[-1] // 2
    x1, x2 = x[..., :half], x[..., half:]
    cos = cos.astype(x.dtype)
    sin = sin.astype(x.dtype)
    return jnp.concatenate([x1 * cos - x2 * sin, x1 * sin + x2 * cos], axis=-1)


def short_conv(x, w, b):
    s = x.shape[1]
    xp = jnp.pad(x, ((0, 0), (RG_CONV_LEFT, RG_CONV_W - 1 - RG_CONV_LEFT), (0, 0)))
    y = b
    for k in range(RG_CONV_W):
        y = y + w[k] * xp[:, k:k + s]
    return y


def blockdiag(x, w, b):
    xb = x.reshape(x.shape[0], x.shape[1], RG_BLOCKS, RG_BLOCK_DIM)
    return jnp.einsum('bsnd,nde->bsne', xb, w).reshape(x.shape) + b


def rglru_coeffs(u, wa, ba, wx, bx, lam):
    u = u.astype(jnp.float32)
    r = jax.nn.sigmoid(blockdiag(u, wa, ba))
    i = jax.nn.sigmoid(blockdiag(u, wx, bx))
    log_a = -RG_C * r * jax.nn.softplus(-lam)
    a = jnp.exp(log_a)
    mult = jnp.sqrt(-jnp.expm1(2.0 * log_a))
    return a, mult * (i * u)


def linear_scan(a, b, reverse):
    def comb(left, right):
        a1, b1 = left
        a2, b2 = right
        return a1 * a2, a2 * b1 + b2
    return lax.associative_scan(comb, (a, b), reverse=reverse, axis=1)[1]


def rglru_branch(pc, pl, conv_w, conv_b, wa, ba, wx, bx, lam, with_ctx_out):
    uc = short_conv(pc['rg_x'], conv_w, conv_b)
    ul = short_conv(pl['rg_x'], conv_w, conv_b)
    hs_c, hs_l = [], []
    for d, rev in enumerate((False, True)):
        a_c, b_c = rglru_coeffs(uc, wa[d], ba[d], wx[d], bx[d], lam[d])
        h_c = linear_scan(a_c, b_c, rev)
        h0 = h_c[:, 0] if rev else h_c[:, -1]
        a_l, b_l = rglru_coeffs(ul, wa[d], ba[d], wx[d], bx[d], lam[d])
        first = -1 if rev else 0
        b_l = b_l.at[:, first].add(a_l[:, first] * h0)
        hs_c.append(h_c)
        hs_l.append(linear_scan(a_l, b_l, rev))
    y_l = (hs_l[0] + hs_l[1]).astype(ul.dtype) * jax.nn.gelu(pl['rg_gate'])
    y_c = (hs_c[0] + hs_c[1]).astype(uc.dtype) * jax.nn.gelu(pc['rg_gate']) if with_ctx_out else None
    return y_c, y_l


def da_heads_qk(t):
    b, s, _ = t.shape
    return t.reshape(b, s, DA_HEADS, 2, DA_HEAD_DIM).transpose(0, 2, 3, 1, 4)


def da_heads_v(t):
    b, s, _ = t.shape
    return t.reshape(b, s, DA_HEADS, 2 * DA_HEAD_DIM).transpose(0, 2, 1, 3)


def diff_softmax_attend(q, k, v, lam):
    s = jnp.einsum('bhjqd,bhjkd->bhjqk', q, k).astype(jnp.float32)
    p = jax.nn.softmax(s, axis=-1)
    w = p[:, :, 0] - lam * p[:, :, 1]
    return jnp.einsum('bhqk,bhkv->bhqv', w.astype(v.dtype), v)


def diff_attn_branch(pc, pl, lq, lk, subln_g, lam_init, cos, sin, with_ctx_out):
    scale = DA_HEAD_DIM ** -0.5
    lam = jnp.exp(jnp.sum(lq[0] * lk[0])) - jnp.exp(jnp.sum(lq[1] * lk[1])) + lam_init
    qc = da_heads_qk(pc['da_q']) * scale
    kc = da_heads_qk(pc['da_k'])
    vc = da_heads_v(pc['da_v'])
    ql = apply_rope(da_heads_qk(pl['da_q']), cos, sin) * scale
    kl = apply_rope(da_heads_qk(pl['da_k']), cos, sin)
    vl = da_heads_v(pl['da_v'])
    keys = jnp.concatenate([kc, kl], axis=3)
    vals = jnp.concatenate([vc, vl], axis=2)
    b, h, _, s, d = ql.shape
    nb = s // Q_BLOCK
    q_blocks = ql.reshape(b, h, 2, nb, Q_BLOCK, d).transpose(3, 0, 1, 2, 4, 5)
    o_l = lax.map(lambda qb: diff_softmax_attend(qb, keys, vals, lam), q_blocks)
    o_l = o_l.transpose(1, 2, 0, 3, 4).reshape(b, h, s, 2 * d)

    def finish(o):
        o = rms_norm(o, subln_g) * (1.0 - lam_init)
        return o.transpose(0, 2, 1, 3).reshape(o.shape[0], o.shape[2], DA_WIDTH)

    y_l = finish(o_l)
    y_c = finish(diff_softmax_attend(qc, kc, vc, lam)) if with_ctx_out else None
    return y_c, y_l


def hg_heads(t):
    b, s, w = t.shape
    return t.reshape(b, s, HG_HEADS, w // HG_HEADS).transpose(0, 2, 1, 3)


def hgrn_chunk_scan(q, logf, k, v, s0):
    b, h, t, _ = q.shape
    dv = v.shape[-1]
    n = t // HG_CHUNK

    def chunks(a):
        return a.reshape(b, h, n, HG_CHUNK, a.shape[-1]).transpose(2, 0, 1, 3, 4)

    tri = jnp.tril(jnp.ones((HG_CHUNK, HG_CHUNK), dtype=bool))[:, :, None]

    def step(s, inp):
        qc, gc, kc, vc = inp
        cum = jnp.cumsum(gc, axis=2)
        rel = cum[:, :, :, None, :] - cum[:, :, None, :, :]
        decay = jnp.exp(jnp.where(tri, rel, -jnp.inf))
        attn = jnp.einsum('bhtk,bhsk,bhtsk->bhts', qc, kc, decay)
        o = jnp.einsum('bhts,bhsv->bhtv', attn, vc) + jnp.einsum('bhtk,bhkv->bhtv', qc * jnp.exp(cum), s)
        c_end = cum[:, :, -1:, :]
        s = jnp.exp(c_end[:, :, 0, :, None]) * s + jnp.einsum('bhsk,bhsv->bhkv', kc * jnp.exp(c_end - cum), vc)
        return s, o

    s_end, o = lax.scan(step, s0, (chunks(q), chunks(logf), chunks(k), chunks(v)))
    return o.transpose(1, 2, 0, 3, 4).reshape(b, h, t, dv), s_end


def hgrn_branch(pc, pl, lb, onorm_g, with_ctx_out):
    bsz = pl['hg_q'].shape[0]
    outs_c, outs_l = [], []
    for d, (fname, rev) in enumerate((('hg_f_fwd', False), ('hg_f_bwd', True))):
        def prep(p):
            z = p[fname].astype(jnp.float32)
            f = lb[d] + (1.0 - lb[d]) * jax.nn.sigmoid(z)
            k = (1.0 - lb[d]) * jax.nn.sigmoid(-z)
            logf = jnp.log(jnp.maximum(f, LOGF_FLOOR))
            ts = [hg_heads(a.astype(jnp.float32)) for a in (p['hg_q'], logf, k, p['hg_i'])]
            return [jnp.flip(a, axis=2) for a in ts] if rev else ts
        s0 = jnp.zeros((bsz, HG_HEADS, HG_DK, HG_DV), jnp.float32)
        o_c, s_c = hgrn_chunk_scan(*prep(pc), s0)
        o_l, _ = hgrn_chunk_scan(*prep(pl), s_c)
        if rev:
            o_c = jnp.flip(o_c, axis=2)
            o_l = jnp.flip(o_l, axis=2)
        outs_c.append(o_c)
        outs_l.append(o_l)

    def finish(o, g):
        o = rms_norm(o, onorm_g)
        return o.transpose(0, 2, 1, 3).reshape(g.shape).astype(g.dtype) * jax.nn.silu(g)

    y_l = finish(outs_l[0] + outs_l[1], pl['hg_g'])
    y_c = finish(outs_c[0] + outs_c[1], pc['hg_g']) if with_ctx_out else None
    return y_c, y_l


def merge_branches(ys, gate_pre, w_branch, b_gate, w_out):
    gp = gate_pre.reshape(gate_pre.shape[0], gate_pre.shape[1], N_BRANCH, D_MODEL) + b_gate
    g = jax.nn.sigmoid(gp.astype(jnp.float32)).astype(gate_pre.dtype)
    merged = g[:, :, 0] * (ys[0] @ w_branch[0])
    for kb in range(1, N_BRANCH):
        merged = merged + g[:, :, kb] * (ys[kb] @ w_branch[kb])
    return merged @ w_out


def token_mixer(hc, hl, w_in, rg_conv_w, rg_conv_b, rg_wa, rg_ba, rg_wx, rg_bx, rg_lambda,
                da_lq, da_lk, da_subln_g, lam_init, hg_lb, hg_onorm_g, w_branch, b_gate, w_out,
                cos, sin, with_ctx_out):
    pc = dict(zip(IN_NAMES, jnp.split(hc @ w_in, IN_OFFSETS, axis=-1)))
    pl = dict(zip(IN_NAMES, jnp.split(hl @ w_in, IN_OFFSETS, axis=-1)))
    rg_c, rg_l = rglru_branch(pc, pl, rg_conv_w, rg_conv_b, rg_wa, rg_ba, rg_wx, rg_bx, rg_lambda, with_ctx_out)
    da_c, da_l = diff_attn_branch(pc, pl, da_lq, da_lk, da_subln_g, lam_init, cos, sin, with_ctx_out)
    hg_c, hg_l = hgrn_branch(pc, pl, hg_lb, hg_onorm_g, with_ctx_out)
    y_l = merge_branches((rg_l, da_l, hg_l), pl['gates'], w_branch, b_gate, w_out)
    y_c = merge_branches((rg_c, da_c, hg_c), pc['gates'], w_branch, b_gate, w_out) if with_ctx_out else None
    return y_c, y_l


def peer_ffn(h, wq, subkeys, u, v):
    t, d = h.shape
    q = (h @ wq).reshape(t, PEER_HEADS, 2, PEER_DQ // 2)
    s = jnp.einsum('thjd,jnd->thjn', q, subkeys).astype(jnp.float32)
    s1, i1 = lax.top_k(s[:, :, 0], PEER_TOPK)
    s2, i2 = lax.top_k(s[:, :, 1], PEER_TOPK)
    cand_s = (s1[..., :, None] + s2[..., None, :]).reshape(t, PEER_HEADS, PEER_TOPK * PEER_TOPK)
    cand_i = (i1[..., :, None] * PEER_NKEYS + i2[..., None, :]).reshape(t, PEER_HEADS, PEER_TOPK * PEER_TOPK)
    top_s, pos = lax.top_k(cand_s, PEER_TOPK)
    idx = jnp.take_along_axis(cand_i, pos, axis=-1)
    w = jax.nn.softmax(top_s, axis=-1).astype(h.dtype)
    nb = t // PEER_TOK_BLOCK

    def block(args):
        hb, ib, wb = args
        act = jax.nn.gelu(jnp.einsum('td,thkd->thk', hb, u[ib]))
        return jnp.einsum('thk,thkd->td', wb * act, v[ib])

    y = lax.map(block, (h.reshape(nb, PEER_TOK_BLOCK, d),
                        idx.reshape(nb, PEER_TOK_BLOCK, PEER_HEADS, PEER_TOPK),
                        w.reshape(nb, PEER_TOK_BLOCK, PEER_HEADS, PEER_TOPK)))
    return y.reshape(t, d)


def setup_inputs(seed: int = 0) -> dict:
    key = jax.random.key(seed)
    ks = jax.random.split(key, 32)
    f32 = jnp.float32
    D = D_MODEL

    def nrm(k, shape, std):
        return jax.random.normal(k, shape, f32) * std

    p_lo, p_hi = 0.9 ** (1.0 / RG_C), 0.999 ** (1.0 / RG_C)
    p = jax.random.uniform(ks[15], (DEPTH, 2, RG_WIDTH), f32, p_lo, p_hi)
    rg_lambda = jnp.log(p) - jnp.log1p(-p)
    return {
        'x': nrm(ks[0], (BATCH, SEQ, D), 1.0),
        'c': nrm(ks[1], (BATCH, D), 1.0),
        'ctx': nrm(ks[2], (BATCH, CTX_LEN, D), 1.0),
        'c_ctx': nrm(ks[3], (D,), 1.0),
        'w_ada': nrm(ks[4], (DEPTH, D, N_MOD * D), 0.5 * D ** -0.5),
        'b_ada': nrm(ks[5], (DEPTH, N_MOD * D), 0.02),
        'norm_mix_g': 1.0 + nrm(ks[6], (DEPTH, D), 0.02),
        'norm_ffn_g': 1.0 + nrm(ks[7], (DEPTH, D), 0.02),
        'w_in': nrm(ks[8], (DEPTH, D, IN_WIDTH), D ** -0.5),
        'rg_conv_w': nrm(ks[9], (DEPTH, RG_CONV_W, RG_WIDTH), RG_CONV_W ** -0.5),
        'rg_conv_b': nrm(ks[10], (DEPTH, RG_WIDTH), 0.02),
        'rg_wa': nrm(ks[11], (DEPTH, 2, RG_BLOCKS, RG_BLOCK_DIM, RG_BLOCK_DIM), RG_BLOCK_DIM ** -0.5),
        'rg_ba': nrm(ks[12], (DEPTH, 2, RG_WIDTH), 0.02),
        'rg_wx': nrm(ks[13], (DEPTH, 2, RG_BLOCKS, RG_BLOCK_DIM, RG_BLOCK_DIM), RG_BLOCK_DIM ** -0.5),
        'rg_bx': nrm(ks[14], (DEPTH, 2, RG_WIDTH), 0.02),
        'rg_lambda': rg_lambda,
        'da_lq': nrm(ks[16], (DEPTH, 2, DA_HEAD_DIM), 0.1),
        'da_lk': nrm(ks[17], (DEPTH, 2, DA_HEAD_DIM), 0.1),
        'da_subln_g': 1.0 + nrm(ks[18], (DEPTH, 2 * DA_HEAD_DIM), 0.02),
        'hg_lb': nrm(ks[19], (2, DEPTH, HG_HEADS * HG_DK), 0.5),
        'hg_onorm_g': 1.0 + nrm(ks[20], (DEPTH, HG_DV), 0.02),
        'w_branch': nrm(ks[21], (DEPTH, N_BRANCH, BRANCH_WIDTH, D), BRANCH_WIDTH ** -0.5),
        'b_gate': nrm(ks[22], (DEPTH, N_BRANCH, D), 0.02),
        'w_out': nrm(ks[23], (DEPTH, D, D), D ** -0.5),
        'peer_wq': nrm(ks[24], (DEPTH, D, PEER_HEADS * PEER_DQ), D ** -0.5),
        'peer_subkeys': nrm(ks[25], (DEPTH, 2, PEER_NKEYS, PEER_DQ // 2), (PEER_DQ // 2) ** -0.5),
        'peer_u': nrm(ks[26], (DEPTH, PEER_N, D), D ** -0.5),
        'peer_v': nrm(ks[27], (DEPTH, PEER_N, D), 0.5 * PEER_HEADS ** -0.5),
        'final_norm_g': 1.0 + nrm(ks[28], (D,), 0.02),
    }


def reference(x, c, ctx, c_ctx, w_ada, b_ada, norm_mix_g, norm_ffn_g, w_in, rg_conv_w, rg_conv_b,
              rg_wa, rg_ba, rg_wx, rg_bx, rg_lambda, da_lq, da_lk, da_subln_g, hg_lb, hg_onorm_g,
              w_branch, b_gate, w_out, peer_wq, peer_subkeys, peer_u, peer_v, final_norm_g):
    bsz, s, d = x.shape
    n_ctx = ctx.shape[1]
    rows = s // GRID_W
    cos, sin = axial_rope_tables(rows)
    lb = jnp.cumsum(jax.nn.softmax(hg_lb.astype(jnp.float32), axis=1), axis=1)
    lb = lb - lb[:, :1]
    sc_l = jax.nn.silu(c)
    sc_c = jax.nn.silu(c_ctx)
    xl, xc = x, ctx
    for l in range(DEPTH):
        last = l == DEPTH - 1
        mod_l = [m[:, None, :] for m in jnp.split(sc_l @ w_ada[l] + b_ada[l], N_MOD, axis=-1)]
        mod_c = jnp.split(sc_c @ w_ada[l] + b_ada[l], N_MOD, axis=-1)
        hl = modulate(rms_norm(xl, norm_mix_g[l]), mod_l[0], mod_l[1])
        hc = modulate(rms_norm(xc, norm_mix_g[l]), mod_c[0], mod_c[1])
        yc, yl = token_mixer(hc, hl, w_in[l], rg_conv_w[l], rg_conv_b[l], rg_wa[l], rg_ba[l], rg_wx[l],
                             rg_bx[l], rg_lambda[l], da_lq[l], da_lk[l], da_subln_g[l],
                             0.8 - 0.6 * math.exp(-0.3 * l), lb[:, l], hg_onorm_g[l], w_branch[l],
                             b_gate[l], w_out[l], cos, sin, not last)
        xl = xl + mod_l[2] * yl
        if not last:
            xc = xc + mod_c[2] * yc
        hl = modulate(rms_norm(xl, norm_ffn_g[l]), mod_l[3], mod_l[4])
        if last:
            yl = peer_ffn(hl.reshape(bsz * s, d), peer_wq[l], peer_subkeys[l], peer_u[l], peer_v[l]).reshape(bsz, s, d)
        else:
            hc = modulate(rms_norm(xc, norm_ffn_g[l]), mod_c[3], mod_c[4])
            h_all = jnp.concatenate([hc, hl], axis=1)
            y_all = peer_ffn(h_all.reshape(-1, d), peer_wq[l], peer_subkeys[l], peer_u[l],
                             peer_v[l]).reshape(bsz, n_ctx + s, d)
            xc = xc + mod_c[5] * y_all[:, :n_ctx]
            yl = y_all[:, n_ctx:]
        xl = xl + mod_l[5] * yl
    return rms_norm(xl, final_norm_g)
```

```python
import contextlib
import numpy as np
import concourse.bass as bass
import concourse.mybir as mybir

F32 = mybir.dt.float32
BF16 = mybir.dt.bfloat16
AF = mybir.ActivationFunctionType
ALU = mybir.AluOpType
AX = mybir.AxisListType


class View:
    __slots__ = ("buf", "ap")

    def __init__(self, buf, ap):
        self.buf = buf
        self.ap = ap

    def __getitem__(self, k):
        return View(self.buf, self.ap[k])

    def r(self, pat, **kw):
        return View(self.buf, self.ap.rearrange(pat, **kw))

    def bc(self, shape):
        return View(self.buf, self.ap.to_broadcast(list(shape)))

    def unsq(self, ax):
        return View(self.buf, self.ap.unsqueeze(ax))

    def pb(self, n):
        return View(self.buf, self.ap.partition_broadcast(n))

    @property
    def shape(self):
        return self.ap.shape


class Buf:
    __slots__ = ("t", "name", "w", "r", "dsem", "dcnt", "dram")

    def __init__(self, t, name, dram=False):
        self.t = t
        self.name = name
        self.w = None
        self.r = []
        self.dsem = None
        self.dcnt = 0
        self.dram = dram

    def __getitem__(self, k):
        return View(self, self.t[k])

    @property
    def v(self):
        return View(self, self.t if self.dram else self.t[:])


def _ap(x):
    return x.ap if isinstance(x, View) else x


def _bufs(xs):
    out = []
    for x in xs:
        if isinstance(x, View) and x.buf not in out:
            out.append(x.buf)
    return out


class Prog:
    ENG = ("pe", "act", "dve", "pool", "sp")

    def __init__(self):
        self.nc = bass.Bass("TRN2", target_bir_lowering=False)
        nc = self.nc
        self.es = contextlib.ExitStack()
        self.es.enter_context(nc.allow_non_contiguous_dma(reason="small strided parameter loads"))
        self.scopes = [self.es]
        self.scope_bufs = [[]]
        self.eng = {"pe": nc.tensor, "act": nc.scalar, "dve": nc.vector,
                    "pool": nc.gpsimd, "sp": nc.sync}
        self.sem = {e: self.es.enter_context(nc.semaphore("s_" + e)) for e in self.ENG}
        self.cnt = {e: 0 for e in self.ENG}
        self.seen = {e: {} for e in self.ENG}
        self.nb = 0
        self.out_waits = []
        self.sem_pool = []
        self.nsem = 0

    def dram_in(self, name, shape, dt=F32):
        return self.nc.dram_tensor(name, list(shape), dt, kind="ExternalInput").ap()

    def dram_out(self, name, shape, dt=F32):
        return self.nc.dram_tensor(name, list(shape), dt, kind="ExternalOutput").ap()

    def sb(self, shape, dt=F32, name=None):
        self.nb += 1
        name = "sb%d_%s" % (self.nb, name or "")
        t = self.scopes[-1].enter_context(self.nc.sbuf_tensor(name, list(shape), dt))
        b = Buf(t, name)
        self.scope_bufs[-1].append(b)
        return b

    def ps(self, shape, dt=F32, name=None):
        self.nb += 1
        name = "ps%d_%s" % (self.nb, name or "")
        t = self.scopes[-1].enter_context(self.nc.psum_tensor(name, list(shape), dt))
        b = Buf(t, name)
        self.scope_bufs[-1].append(b)
        return b

    def dram(self, shape, dt=F32, name=None, addr_space="Local"):
        self.nb += 1
        name = "dr%d_%s" % (self.nb, name or "")
        t = self.nc.dram_tensor(name, list(shape), dt, kind="Internal", addr_space=addr_space)
        return Buf(t.ap(), name, dram=True)

    @contextlib.contextmanager
    def scope(self):
        st = contextlib.ExitStack()
        self.scopes.append(st)
        self.scope_bufs.append([])
        try:
            yield
        finally:
            bufs = self.scope_bufs.pop()
            self.barrier(bufs)
            for b in bufs:
                if b.dsem is not None:
                    self.sem_pool.append((b.dsem, b.dcnt))
                    b.dsem = None
            self.scopes.pop()
            st.close()

    def barrier(self, bufs=()):
        for e in self.ENG:
            for f in self.ENG:
                if f != e and self.cnt[f] > 0:
                    self._wait(e, ("eng", f, self.cnt[f]))
            for b in bufs:
                if b.dsem is not None and b.dcnt > 0:
                    self._wait(e, ("dma", b, b.dcnt))

    def _getsem(self, b):
        if b.dsem is None:
            if self.sem_pool and not b.dram:
                b.dsem, b.dcnt = self.sem_pool.pop()
            else:
                self.nsem += 1
                b.dsem = self.es.enter_context(self.nc.semaphore("d%d" % self.nsem))
                b.dcnt = 0
        return b.dsem

    def _wait(self, e, dep):
        kind, key, val = dep
        if kind == "eng":
            if key == e and e == "pe":
                return
            sem = self.sem[key]
            skey = key
        else:
            sem = key.dsem
            if sem is None:
                return
            skey = "d:%d" % id(sem)
        if self.seen[e].get(skey, 0) >= val:
            return
        self.eng[e].wait_ge(sem, val)
        self.seen[e][skey] = val

    def _deps(self, e, reads, writes, dma=False):
        for b in reads:
            if b.w is not None:
                self._wait(e, b.w)
        for b in writes:
            if b.w is not None:
                if not (dma and b.w[0] == "dma"):
                    self._wait(e, b.w)
            for d in b.r:
                self._wait(e, d)

    def op(self, e, ins, reads=(), writes=()):
        reads = _bufs(reads)
        writes = _bufs(writes)
        self._deps(e, reads, writes)
        inst = ins(self.eng[e])
        self.cnt[e] += 1
        inst.then_inc(self.sem[e], 1)
        dep = ("eng", e, self.cnt[e])
        for b in reads:
            b.r = [d for d in b.r if not (d[0] == "eng" and d[1] == e)] + [dep]
        for b in writes:
            b.w = dep
            b.r = []
        return inst

    def dma(self, q, out, in_, **kw):
        reads = _bufs([in_])
        writes = _bufs([out])
        self._deps(q, reads, writes, dma=True)
        bufs = reads + writes
        own = [x for x in bufs if not x.dram]
        b = own[0] if own else bufs[0]
        self._getsem(b)
        inst = self.eng[q].dma_start(out=_ap(out), in_=_ap(in_), **kw)
        b.dcnt += 16
        inst.then_inc(b.dsem, 16)
        dep = ("dma", b, b.dcnt)
        for x in reads:
            x.r = [d for d in x.r if not (d[0] == "dma" and d[1] is b)] + [dep]
        for x in writes:
            x.w = dep
            x.r = []
        if not writes or any(x.dram for x in writes):
            self.out_waits.append(dep)
        return inst

    def allgather(self, src, dst, n=8):
        self._deps("pool", [src], [dst])
        self._getsem(dst)
        inst = self.nc.gpsimd.collective_compute(
            "AllGather", op=ALU.bypass, replica_groups=[list(range(n))],
            ins=[src.t.opt()], outs=[dst.t.opt()])
        dst.dcnt += 1
        inst.then_inc(dst.dsem, 1)
        dep = ("dma", dst, dst.dcnt)
        src.r = src.r + [dep]
        dst.w = dep
        dst.r = []
        self.out_waits.append(dep)
        return inst

    def finish(self):
        last = {}
        for kind, b, val in self.out_waits:
            if b.dsem is None:
                continue
            k = id(b)
            if k not in last or last[k][2] < val:
                last[k] = (kind, b, val)
        for dep in last.values():
            if dep[2] <= dep[1].dcnt:
                self._wait("sp", dep)
        for e in self.ENG:
            if e != "sp" and self.cnt[e] > 0:
                self._wait("sp", ("eng", e, self.cnt[e]))
        self.es.close()
        return self.nc

    def mm(self, out, lhsT, rhs, start=True, stop=True):
        return self.op("pe", lambda E: E.matmul(out.ap, lhsT.ap, rhs.ap, start=start, stop=stop),
                       reads=[lhsT, rhs], writes=[out])

    def tr(self, out, in_, ident):
        return self.op("pe", lambda E: E.transpose(out.ap, in_.ap, ident.ap), reads=[in_, ident], writes=[out])

    def tt(self, e, out, a, b, op):
        return self.op(e, lambda E: E.tensor_tensor(out=out.ap, in0=a.ap, in1=b.ap, op=op), reads=[a, b], writes=[out])

    def ts(self, e, out, a, s1, op0, s2=None, op1=None, accum=None):
        kw = {}
        if op1 is not None:
            kw["op1"] = op1
        if accum is not None:
            kw["accum_out"] = accum.ap
        return self.op(e, lambda E: E.tensor_scalar(out=out.ap, in0=a.ap, scalar1=_ap(s1), scalar2=_ap(s2), op0=op0, **kw),
                       reads=[a, s1, s2], writes=[out] + ([accum] if accum is not None else []))

    def stt(self, out, a, s, b, op0, op1):
        return self.op("dve", lambda E: E.scalar_tensor_tensor(out=out.ap, in0=a.ap, scalar=_ap(s), in1=b.ap, op0=op0, op1=op1),
                       reads=[a, s, b], writes=[out])

    def act(self, out, a, func, bias=None, scale=1.0, accum=None):
        kw = {}
        if bias is not None:
            kw["bias"] = _ap(bias)
        if accum is not None:
            kw["accum_out"] = accum.ap
        return self.op("act", lambda E: E.activation(out=out.ap, in_=a.ap, func=func, scale=_ap(scale), **kw),
                       reads=[a, bias, scale], writes=[out] + ([accum] if accum is not None else []))

    def cp(self, e, out, a):
        if e == "act":
            return self.op("act", lambda E: E.copy(out=out.ap, in_=a.ap), reads=[a], writes=[out])
        return self.op(e, lambda E: E.tensor_copy(out=out.ap, in_=a.ap), reads=[a], writes=[out])

    def memset(self, e, out, val):
        return self.op(e, lambda E: E.memset(out.ap, val), writes=[out])

    def recip(self, out, a):
        return self.op("dve", lambda E: E.reciprocal(out=out.ap, in_=a.ap), reads=[a], writes=[out])


import math

NCORE = 8
D = 2048
NT = 1088
TB = 4352
EPS = 1e-6
TG = [(0, 512), (512, 512), (1024, 64)]
GRP = ["rgx", "rgg", "q", "k", "qsw", "ksw", "hq", "hg", "zf", "zb", "v", "hv"]
GI = {n: i for i, n in enumerate(GRP)}


def tiles_of(n, step=128):
    return [(s, min(step, n - s)) for s in range(0, n, step)]


class MK:
    def __init__(self, layers=(0, 1, 2, 3), stages=None, dbg=()):
        self.P = Prog()
        self.layers = layers
        self.stages = stages
        self.dbg = dbg
        P = self.P
        NL = len(layers)
        self.lidx = {l: i for i, l in enumerate(layers)}
        shapes = {}

        def inp(name, shape):
            shapes[name] = shape
        inp("x", [NT, D]); inp("cvT", [128, 16, 3]); inp("wada", [NL, D, 1536]); inp("bada", [NL, 1536])
        inp("gmix", [NL, D]); inp("gffn", [NL, D]); inp("gfin", [D])
        inp("wc", [NL, D, 1536]); inp("wg", [NL, 256, 6144]); inp("bgT", [128, NL, 48])
        inp("rgcw", [128, NL, 4]); inp("rgcb", [128, NL]); inp("rgwa", [NL, 2, 128, 128]); inp("rgwx", [NL, 2, 128, 128])
        inp("rgba", [128, NL, 2]); inp("rgbx", [128, NL, 2]); inp("rglam", [128, NL, 2])
        inp("dalq", [NL, 128]); inp("dalk", [NL, 128]); inp("dasg", [128, NL])
        inp("hglbF", [128, 2, 4]); inp("hglbT", [2, 4, 128]); inp("hgon", [128, NL])
        inp("cosF", [128, 4096]); inp("sinF", [128, 4096])
        inp("wbr", [NL, 384, D]); inp("wout", [NL, 256, D]); inp("wq", [NL, 256, D]); inp("ut", [NL, 256, 16384])
        inp("vv", [NL, 2048, D]); inp("skT", [NL, 2, 128, 128])
        inp("ident", [128, 128]); inp("hgc", [12, 128, 128]); inp("ci", [128, 2])
        for k in list(shapes):
            shapes["inj_" + k] = None

        class Lazy(dict):
            def __missing__(d, name):
                d[name] = P.dram_in(name, shapes[name])
                return d[name]
        I = self.I = Lazy()
        self.shapes = shapes
        self.out = P.dram_out("out", [1024, D])
        self.dbg_out = {}

        self.pid = P.nc.gpsimd.partition_id()
        self.off384 = self.pid * 384
        S = self.S = {}
        S["xres"] = P.dram([NT, D], name="xres")
        S["mods_loc"] = P.dram([12, 1536], name="mods_loc")
        S["mods_all"] = P.dram([96, 1536], name="mods_all")
        S["hT_loc"] = P.dram([D, NT], name="hT_loc")
        S["hT_all"] = P.dram([8 * D, NT], name="hT_all")
        S["gatesT"] = P.dram([6144, NT], name="gatesT")
        for n in ("rgx", "rgg", "q", "k", "hq", "hg", "zfF", "zbF"):
            S[n] = P.dram([128, 2 * TB], name=n)
        for n in ("v", "zf", "zb", "hv"):
            S[n] = P.dram([2 * TB, 128], name=n)
        S["y_loc"] = P.dram([8 * 384, NT], name="y_loc")
        S["y_all"] = P.dram([8 * 3072, NT], name="y_all")
        S["y_in"] = P.dram([3072, NT], name="y_in")
        S["mT"] = P.dram([D, NT], name="mT")
        S["h2T"] = P.dram([D, NT], name="h2T")
        S["scores"] = P.dram([NT, 2048], name="scores")
        S["GT"] = P.dram([16384, NT], name="GT")
        S["WT"] = P.dram([16384, NT], name="WT")
        C = self.C = {}
        C["ident"] = P.sb([128, 128], name="ident")
        P.dma("sp", C["ident"].v, I["ident"])
        C["ones"] = P.sb([128, 128], name="ones")
        P.memset("dve", C["ones"].v, 1.0)

    def tap(self, name, buf):
        if name in self.dbg:
            o = self.P.dram_out("dbg_" + name, list(buf.t.shape))
            self.P.dma("sp", o, buf.v)

    def gather_weights(self, l):
        P, I = self.P, self.I
        W = {}
        for nm, rows, cols in (("wg", 256, 6144), ("wbr", 384, D), ("wout", 256, D), ("wq", 256, D),
                               ("ut", 256, 16384), ("vv", 2048, D)):
            bnc = P.dram([rows, cols], name="bn_%s%d" % (nm, l))
            full = P.dram([8 * rows, cols], name="g_%s%d" % (nm, l))
            npc = 4
            step = rows // npc
            for i in range(npc):
                P.dma("pool", bnc[i * step:(i + 1) * step, :], I[nm][self.lidx[l], i * step:(i + 1) * step, :])
            P.allgather(bnc, full)
            W[nm] = full
        return W

    def load_modvec(self, dst, l, kind, i):
        P = self.P
        ma = self.S["mods_all"]
        c0 = i * 2048
        pos = c0
        while pos < c0 + 2048:
            r = pos // 1536
            e = min((r + 1) * 1536, c0 + 2048)
            src = ma[r * 12 + self.lidx[l] * 3 + kind, pos - r * 1536:e - r * 1536].pb(128)
            P.dma("act", dst[:, pos - c0:e - c0], src)
            pos = e

    def emit_lmod(self):
        P, I, S = self.P, self.I, self.S
        with P.scope():
            sc = P.sb([128, 16, 3]); scs = P.sb([128, 16, 3])
            P.dma("sp", sc.v, I["cvT"])
            P.act(scs.v, sc.v, AF.Silu)
            wb = [P.sb([128, 16, 512], name="w%d" % i) for i in range(2)]
            bb = [P.sb([3, 512], name="bb%d" % i) for i in range(2)]
            ob = [P.sb([3, 512], name="ob%d" % i) for i in range(2)]
            pss = [P.ps([3, 512], name="ps%d" % i) for i in range(2)]
            it = 0
            for l in range(len(self.layers)):
                for n in range(3):
                    i = it % 2; it += 1
                    src = I["wada"][l].rearrange("(k p) c -> p k c", p=128)[:, :, n * 512:(n + 1) * 512]
                    for h in range(2):
                        P.dma("sp", wb[i][:, h * 8:(h + 1) * 8, :], src[:, h * 8:(h + 1) * 8, :])
                    P.dma("sp", bb[i].v, I["bada"][l, n * 512:(n + 1) * 512].partition_broadcast(3))
                    for k in range(16):
                        P.mm(pss[i].v, scs[:, k, :], wb[i][:, k, :], start=(k == 0), stop=(k == 15))
                    P.tt("dve", ob[i].v, pss[i].v, bb[i].v, ALU.add)
                    P.dma("sp", S["mods_loc"][l * 3:(l + 1) * 3, n * 512:(n + 1) * 512], ob[i].v)
        P.allgather(S["mods_loc"], S["mods_all"])
        self.tap("mods_all", S["mods_all"])

    def norm_rows(self, xt, rows, h, ss, A, B=None):
        P = self.P
        P.act(h, xt, AF.Square, accum=ss)
        P.ts("dve", ss, ss, 1.0 / D, ALU.mult, EPS, ALU.add)
        P.act(ss, ss, AF.Sqrt)
        P.recip(ss, ss)
        P.stt(h, xt, ss, A, ALU.mult, ALU.mult)
        if B is not None:
            P.tt("dve", h, h, B, ALU.add)

    def make_AB(self, l, gname, i_shift, i_scale, tmp):
        P, I = self.P, self.I
        P.dma("act", tmp.v, I[gname][self.lidx[l]].partition_broadcast(128))
        out = {}
        for kind in range(3):
            A = P.sb([128, D], name="A%d" % kind)
            B = P.sb([128, D], name="B%d" % kind)
            self.load_modvec(A.v, l, kind, i_scale)
            self.load_modvec(B.v, l, kind, i_shift)
            P.stt(A.v, A.v, 1.0, tmp.v, ALU.add, ALU.mult)
            out[kind] = (A, B)
        return out


def kind_of_tile(t0):
    return 0 if t0 < 512 else (1 if t0 < 1024 else 2)


class MK2(MK):
    def emit_t1a(self, l):
        P, I, S, C = self.P, self.I, self.S, self.C
        with P.scope():
            xt = [P.sb([128, D], name="xt%d" % i) for i in range(2)]
            h = P.sb([128, D], name="h")
            AB = self.make_AB(l, "gmix", 0, 1, xt[0])
            ss = [P.sb([128, 1], name="ss%d" % i) for i in range(2)]
            hTt = [P.sb([128, 16, 128], name="hTt%d" % i) for i in range(2)]
            pst = [P.ps([128, 512], name="pst%d" % i) for i in range(2)]
            ntr = 0
            for ti, (t0, rows) in enumerate(tiles_of(NT)):
                x_ = xt[ti % 2]; s_ = ss[ti % 2]; hT = hTt[ti % 2]
                P.dma("sp", x_[:rows, :], S["xres"][t0:t0 + rows, :])
                A, B = AB[kind_of_tile(t0)]
                self.norm_rows(x_[:rows, :], rows, h[:rows, :], s_[:rows, :], A[:rows, :], B[:rows, :])
                for q in range(4):
                    pt = pst[ntr % 2]; ntr += 1
                    for kk in range(4):
                        k = q * 4 + kk
                        P.tr(pt[:, kk * 128:kk * 128 + rows], h[:rows, k * 128:(k + 1) * 128], C["ident"][:rows, :rows])
                    P.cp("act" if q % 2 == 0 else "dve", hT[:, q * 4:(q + 1) * 4, :rows],
                         pt.v.r("p (a b) -> p a b", a=4)[:, :, :rows])
                P.dma("sp", S["hT_loc"].v.r("(k p) t -> p k t", p=128)[:, :, t0:t0 + rows], hT[:, :, :rows])
        P.allgather(S["hT_loc"], S["hT_all"])
        self.tap("hT_all", S["hT_all"])

    def emit_t1b(self, l, W):
        P, I, S, C = self.P, self.I, self.S, self.C
        with P.scope():
            hT = P.sb([128, 16, NT], name="hT")
            for q in range(4):
                P.dma("sp", hT[:, q * 4:(q + 1) * 4, :], S["hT_loc"].v.r("(k p) t -> p k t", p=128)[:, q * 4:(q + 1) * 4, :])
            bg = P.sb([128, 48], name="bg")
            P.dma("act", bg.v, I["bgT"][:, self.lidx[l], :])
            wb = [P.sb([128, 16, 512], name="w%d" % i) for i in range(2)]
            st = [P.sb([128, 512], name="st%d" % i) for i in range(4)]
            pm = [P.ps([128, 512], name="pm%d" % i) for i in range(4)]
            wsrc = W["wg"].v.r("(k p) c -> p k c", p=128)

            def loadw(n):
                for hh in range(4):
                    P.dma("sp", wb[n % 2][:, hh * 4:(hh + 1) * 4, :], wsrc[:, hh * 4:(hh + 1) * 4, n * 512:(n + 1) * 512])
            loadw(0)
            it = 0
            for n in range(12):
                if n + 1 < 12:
                    loadw(n + 1)
                for c4 in range(4):
                    ch = n * 4 + c4
                    for (t0, tn) in TG:
                        p_ = pm[it % 4]; s_ = st[it % 4]; it += 1
                        for k in range(16):
                            P.mm(p_[:, :tn], wb[n % 2][:, k, c4 * 128:(c4 + 1) * 128], hT[:, k, t0:t0 + tn],
                                 start=(k == 0), stop=(k == 15))
                        P.act(s_[:, :tn], p_[:, :tn], AF.Sigmoid, bias=bg[:, ch:ch + 1])
                        P.dma("sp", S["gatesT"][ch * 128:(ch + 1) * 128, t0:t0 + tn], s_[:, :tn])
        self.tap("gatesT", S["gatesT"])

    def emit_t1c(self, l):
        P, I, S, C = self.P, self.I, self.S, self.C
        with P.scope():
            wc = P.sb([128, 16, 1536], name="wc")
            wsrc = I["wc"][self.lidx[l]].rearrange("(k p) c -> p k c", p=128)
            for q in range(8):
                P.dma("sp", wc[:, q * 2:(q + 1) * 2, :], wsrc[:, q * 2:(q + 1) * 2, :])
            hTb = [P.sb([128, 16, 512], name="hTb%d" % i) for i in range(2)]
            cs = [P.sb([128, 512], name="cs%d" % i) for i in range(2)]
            sn = [P.sb([128, 512], name="sn%d" % i) for i in range(2)]
            st = [P.sb([128, 512], name="st%d" % i) for i in range(4)]
            tmp = [P.sb([128, 512], name="tmp%d" % i) for i in range(2)]
            pm = [P.ps([128, 512], name="pm%d" % i) for i in range(6)]
            it = 0
            ic = 0
            Fdst = {"rgx": "rgx", "rgg": "rgg", "hq": "hq", "hg": "hg", "zf": "zfF", "zb": "zbF"}
            for r in range(8):
                for ci, (t0, tn) in enumerate(TG):
                    hb = hTb[ic % 2]; cb = cs[ic % 2]; sb_ = sn[ic % 2]; ic += 1
                    src = S["hT_all"].v[r * D:(r + 1) * D, t0:t0 + tn].r("(k p) t -> p k t", p=128)
                    for q in range(4):
                        P.dma("sp", hb[:, q * 4:(q + 1) * 4, :tn], src[:, q * 4:(q + 1) * 4, :])
                    if ci < 2:
                        P.dma("act", cb.v, I["cosF"][:, r * 512:(r + 1) * 512])
                        P.dma("act", sb_.v, I["sinF"][:, r * 512:(r + 1) * 512])
                        segs = [(0, tn, ci * TB + 256 + r * 512)]
                    else:
                        segs = [(0, 32, r * 32), (32, 32, TB + r * 32)]

                    def fmm(gname):
                        nonlocal it
                        p_ = pm[it % 6]; it += 1
                        g0 = GI[gname] * 128
                        for k in range(16):
                            P.mm(p_[:, :tn], wc[:, k, g0:g0 + 128], hb[:, k, :tn], start=(k == 0), stop=(k == 15))
                        return p_

                    for gname in ("rgx", "rgg", "hq", "hg", "zf", "zb"):
                        p_ = fmm(gname)
                        s_ = st[it % 4]
                        P.cp("act" if it % 2 == 0 else "dve", s_[:, :tn], p_[:, :tn])
                        for (c0, n, d0) in segs:
                            P.dma("sp", S[Fdst[gname]][:, d0:d0 + n], s_[:, c0:c0 + n])
                    for gname, gsw in (("q", "qsw"), ("k", "ksw")):
                        p1 = fmm(gname)
                        s_ = st[it % 4]
                        if ci < 2:
                            p2 = fmm(gsw)
                            P.tt("dve", tmp[0][:, :tn], p1[:, :tn], cb[:, :tn], ALU.mult)
                            P.tt("dve", tmp[1][:, :tn], p2[:, :tn], sb_[:, :tn], ALU.mult)
                            P.tt("pool", s_[:, :tn], tmp[0][:, :tn], tmp[1][:, :tn], ALU.add)
                        else:
                            P.cp("act", s_[:, :tn], p1[:, :tn])
                        for (c0, n, d0) in segs:
                            P.dma("sp", S[gname][:, d0:d0 + n], s_[:, c0:c0 + n])
                    for (a0, rows) in tiles_of(tn):
                        p_ = pm[it % 6]; it += 1
                        s_ = st[it % 4]
                        for k in range(16):
                            P.mm(p_[:rows, :], hb[:, k, a0:a0 + rows], wc[:, k, 8 * 128:12 * 128], start=(k == 0), stop=(k == 15))
                        P.cp("act" if it % 2 == 0 else "dve", s_[:rows, :], p_[:rows, :])
                        for (c0, n, d0) in segs:
                            lo = max(c0, a0); hi = min(c0 + n, a0 + rows)
                            if lo >= hi:
                                continue
                            for gi, dn in enumerate(("zf", "zb", "v", "hv")):
                                P.dma("sp", S[dn][d0 + lo - c0:d0 + hi - c0, :], s_[lo - a0:hi - a0, gi * 128:(gi + 1) * 128])
        for n in ("rgx", "q", "k", "v", "zf", "zfF", "hq"):
            self.tap(n, S[n])


def col_groups(n, step=512):
    return [(s, min(step, n - s)) for s in range(0, n, step)]


class MK3(MK2):
    def emit_rg(self, l):
        P, I, S, C = self.P, self.I, self.S, self.C
        with P.scope():
            cw = P.sb([128, 4], name="cw"); cb = P.sb([128, 1], name="cb")
            ba = P.sb([128, 2], name="ba"); bx = P.sb([128, 2], name="bx"); lam = P.sb([128, 2], name="lam")
            nsp8 = P.sb([128, 2], name="nsp8")
            wa = P.sb([128, 2, 128], name="wa"); wx = P.sb([128, 2, 128], name="wx")
            P.dma("act", cw.v, I["rgcw"][:, self.lidx[l], :]); P.dma("act", cb.v, I["rgcb"][:, self.lidx[l]:self.lidx[l] + 1])
            P.dma("act", ba.v, I["rgba"][:, self.lidx[l], :]); P.dma("act", bx.v, I["rgbx"][:, self.lidx[l], :])
            P.dma("act", lam.v, I["rglam"][:, self.lidx[l], :])
            P.dma("act", wa.v, I["rgwa"][self.lidx[l]].rearrange("d p e -> p d e"))
            P.dma("act", wx.v, I["rgwx"][self.lidx[l]].rearrange("d p e -> p d e"))
            P.act(nsp8.v, lam.v, AF.Exp, scale=-1.0)
            P.ts("dve", nsp8.v, nsp8.v, 1.0, ALU.add)
            P.act(nsp8.v, nsp8.v, AF.Ln)
            P.ts("dve", nsp8.v, nsp8.v, -8.0, ALU.mult)
            x = P.sb([128, TB], name="x"); g = P.sb([128, TB], name="g"); u = P.sb([128, TB], name="u")
            ra = P.sb([128, TB], name="ra"); ix = P.sb([128, TB], name="ix"); m = P.sb([128, TB], name="m")
            hf = P.sb([128, TB], name="hf"); hb = P.sb([128, TB], name="hb")
            pp = [P.ps([128, 512], name="pp%d" % i) for i in range(4)]
            it = 0
            for b in range(2):
                for q in range(2):
                    h0 = q * (TB // 2)
                    P.dma("sp", x[:, h0:h0 + TB // 2], S["rgx"][:, b * TB + h0:b * TB + h0 + TB // 2])
                    P.dma("sp", g[:, h0:h0 + TB // 2], S["rgg"][:, b * TB + h0:b * TB + h0 + TB // 2])
                for (a, e) in ((0, 256), (256, TB)):
                    P.ts("dve", u[:, a:e], x[:, a:e], cw[:, 2:3], ALU.mult, cb[:, 0:1], ALU.add)
                    P.stt(u[:, a + 2:e], x[:, a:e - 2], cw[:, 0:1], u[:, a + 2:e], ALU.mult, ALU.add)
                    P.stt(u[:, a + 1:e], x[:, a:e - 1], cw[:, 1:2], u[:, a + 1:e], ALU.mult, ALU.add)
                    P.stt(u[:, a:e - 1], x[:, a + 1:e], cw[:, 3:4], u[:, a:e - 1], ALU.mult, ALU.add)
                for d in range(2):
                    for (c0, n) in col_groups(TB):
                        p1 = pp[it % 4]; it += 1
                        P.mm(p1[:, :n], wa[:, d, :], u[:, c0:c0 + n])
                        P.act(ra[:, c0:c0 + n], p1[:, :n], AF.Sigmoid, bias=ba[:, d:d + 1])
                        p2 = pp[it % 4]; it += 1
                        P.mm(p2[:, :n], wx[:, d, :], u[:, c0:c0 + n])
                        P.act(ix[:, c0:c0 + n], p2[:, :n], AF.Sigmoid, bias=bx[:, d:d + 1])
                    P.act(ra.v, ra.v, AF.Exp, scale=nsp8[:, d:d + 1])
                    P.tt("dve", m.v, ra.v, ra.v, ALU.mult)
                    P.ts("dve", m.v, m.v, -1.0, ALU.mult, 1.0, ALU.add)
                    P.act(m.v, m.v, AF.Sqrt)
                    P.tt("dve", m.v, m.v, ix.v, ALU.mult)
                    P.tt("dve", m.v, m.v, u.v, ALU.mult)
                    if d == 0:
                        init = 0.0
                        for (c0, n) in col_groups(TB, 2048):
                            iv = init
                            P.op("dve", lambda E: E.tensor_tensor_scan(out=hf[:, c0:c0 + n].ap, data0=ra[:, c0:c0 + n].ap,
                                                                        data1=m[:, c0:c0 + n].ap, initial=_ap(iv),
                                                                        op0=ALU.mult, op1=ALU.add),
                                 reads=[ra.v, m.v, hf.v], writes=[hf.v])
                            init = hf[:, c0 + n - 1:c0 + n]
                    else:
                        init = 0.0
                        for (a, e) in ((0, 256), (2304, TB), (256, 2304)):
                            iv = init
                            P.op("dve", lambda E: E.tensor_tensor_scan(out=hb[:, a:e][:, ::-1].ap, data0=ra[:, a:e][:, ::-1].ap,
                                                                        data1=m[:, a:e][:, ::-1].ap, initial=_ap(iv),
                                                                        op0=ALU.mult, op1=ALU.add),
                                 reads=[ra.v, m.v, hb.v], writes=[hb.v])
                            init = hb[:, a:a + 1]
                P.tt("dve", hf.v, hf.v, hb.v, ALU.add)
                P.act(g.v, g.v, AF.Gelu_apprx_tanh)
                P.tt("dve", hf.v, hf.v, g.v, ALU.mult)
                self.store_y(0, b, hf.v)

    def emit_da(self, l):
        P, I, S, C = self.P, self.I, self.S, self.C
        li = 0.8 - 0.6 * math.exp(-0.3 * l)
        with P.scope():
            lq = P.sb([128, 128], name="lq"); lk = P.sb([128, 128], name="lk")
            P.dma("act", lq.v, I["dalq"][self.lidx[l]].partition_broadcast(128))
            P.dma("act", lk.v, I["dalk"][self.lidx[l]].partition_broadcast(128))
            P.tt("dve", lq.v, lq.v, lk.v, ALU.mult)
            e2 = P.sb([128, 2], name="e2")
            P.op("dve", lambda E: E.tensor_reduce(out=e2.v.ap, in_=lq.v.r("p (j d) -> p j d", j=2).ap, op=ALU.add, axis=AX.X),
                 reads=[lq.v], writes=[e2.v])
            P.act(e2.v, e2.v, AF.Exp)
            nlam = P.sb([128, 1], name="nlam")
            P.tt("dve", nlam.v, e2[:, 1:2], e2[:, 0:1], ALU.subtract)
            P.ts("dve", nlam.v, nlam.v, -li, ALU.add)
            gs = P.sb([128, 1], name="gs")
            P.dma("act", gs.v, I["dasg"][:, self.lidx[l]:self.lidx[l] + 1])
            P.ts("dve", gs.v, gs.v, 1.0 - li, ALU.mult)
            kT = P.sb([128, TB], name="kT"); vb = P.sb([128, 34, 128], name="vb")
            qT = [P.sb([128, 512], name="qT%d" % i) for i in range(2)]
            pT = [P.sb([128, 512], name="pT%d" % i) for i in range(3)]
            o0 = P.sb([128, 512], name="o0"); o1 = P.sb([128, 512], name="o1"); rd = P.sb([128, 512], name="rd")
            yo = [P.sb([128, 512], name="yo%d" % i) for i in range(2)]
            psS = [P.ps([128, 512], name="psS%d" % i) for i in range(2)]
            psO = [P.ps([128, 512], name="psO%d" % i) for i in range(2)]
            psD = [P.ps([128, 512], name="psD%d" % i) for i in range(2)]
            psM = P.ps([128, 512], name="psM")
            iq = 0; ip = 0; iss = 0
            for b in range(2):
                for q in range(2):
                    h0 = q * (TB // 2)
                    P.dma("sp", kT[:, h0:h0 + TB // 2], S["k"][:, b * TB + h0:b * TB + h0 + TB // 2])
                P.dma("sp", vb.v, S["v"][b * TB:(b + 1) * TB, :].r("(t p) d -> p t d", p=128))
                qgs = [(0, 256, 2)] + [(256 + i * 512, 512, 34) for i in range(8)]
                for (q0, qn, nkt) in qgs:
                    qt = qT[iq % 2]; iq += 1
                    P.dma("sp", qt[:, :qn], S["q"][:, b * TB + q0:b * TB + q0 + qn])
                    for kt in range(nkt):
                        for j in range(2):
                            ps = psS[iss % 2]; iss += 1
                            P.mm(ps[:, :qn], kT[j * 64:(j + 1) * 64, kt * 128:(kt + 1) * 128], qt[j * 64:(j + 1) * 64, :qn])
                            pt = pT[ip % 3]; ip += 1
                            P.act(pt[:, :qn], ps[:, :qn], AF.Exp, scale=0.125)
                            P.mm(psO[j][:, :qn], vb[:, kt, :], pt[:, :qn], start=(kt == 0), stop=(kt == nkt - 1))
                            P.mm(psD[j][:, :qn], C["ones"].v, pt[:, :qn], start=(kt == 0), stop=(kt == nkt - 1))
                    P.recip(rd[:, :qn], psD[0][:, :qn])
                    P.tt("dve", o0[:, :qn], psO[0][:, :qn], rd[:, :qn], ALU.mult)
                    P.recip(rd[:, :qn], psD[1][:, :qn])
                    P.tt("dve", o1[:, :qn], psO[1][:, :qn], rd[:, :qn], ALU.mult)
                    P.stt(o0[:, :qn], o1[:, :qn], nlam[:, 0:1], o0[:, :qn], ALU.mult, ALU.add)
                    P.tt("pool", o1[:, :qn], o0[:, :qn], o0[:, :qn], ALU.mult)
                    P.mm(psM[:, :qn], C["ones"].v, o1[:, :qn])
                    P.ts("dve", rd[:, :qn], psM[:, :qn], 1.0 / 128, ALU.mult, EPS, ALU.add)
                    P.act(rd[:, :qn], rd[:, :qn], AF.Sqrt)
                    P.recip(rd[:, :qn], rd[:, :qn])
                    y_ = yo[iq % 2]
                    P.stt(y_[:, :qn], o0[:, :qn], gs[:, 0:1], rd[:, :qn], ALU.mult, ALU.mult)
                    self.store_y(1, b, y_[:, :qn], q0, qn)

    def emit_hg(self, l):
        P, I, S, C = self.P, self.I, self.S, self.C
        with P.scope():
            hc = P.sb([128, 12, 128], name="hgc")
            P.dma("act", hc.v, I["hgc"].rearrange("m p t -> p m t"))
            cib = P.sb([128, 2], name="ci")
            P.dma("act", cib.v, I["ci"])
            gon = P.sb([128, 1], name="gon")
            P.dma("act", gon.v, I["hgon"][:, self.lidx[l]:self.lidx[l] + 1])
            lbF = P.sb([128, 2], name="lbF"); omlF = P.sb([128, 2], name="omlF")
            lbT = P.sb([128, 2, 128], name="lbT"); omlT = P.sb([128, 2, 128], name="omlT")
            if l == 0:
                P.memset("dve", lbF.v, 0.0); P.memset("dve", lbT.v, 0.0)
            else:
                eF = P.sb([128, 2, 4], name="eF"); tF = P.sb([128, 2], name="tF")
                P.dma("act", eF.v, I["hglbF"])
                P.act(eF.v, eF.v, AF.Exp)
                P.tt("dve", tF.v, eF[:, :, 0], eF[:, :, 1], ALU.add)
                P.tt("dve", tF.v, tF.v, eF[:, :, 2], ALU.add)
                P.tt("dve", tF.v, tF.v, eF[:, :, 3], ALU.add)
                P.recip(tF.v, tF.v)
                P.cp("dve", lbF.v, eF[:, :, 1])
                for l2 in range(2, l + 1):
                    P.tt("dve", lbF.v, lbF.v, eF[:, :, l2], ALU.add)
                P.tt("dve", lbF.v, lbF.v, tF.v, ALU.mult)
                eT = P.sb([128, 2, 4, 128], name="eT"); tT = P.sb([128, 2, 128], name="tT")
                P.dma("act", eT.v.r("p a b c -> p (a b c)"), I["hglbT"].rearrange("a b c -> (a b c)").partition_broadcast(128))
                P.act(eT.v, eT.v, AF.Exp)
                P.tt("dve", tT.v, eT[:, :, 0, :], eT[:, :, 1, :], ALU.add)
                P.tt("dve", tT.v, tT.v, eT[:, :, 2, :], ALU.add)
                P.tt("dve", tT.v, tT.v, eT[:, :, 3, :], ALU.add)
                P.recip(tT.v, tT.v)
                P.cp("dve", lbT.v, eT[:, :, 1, :])
                for l2 in range(2, l + 1):
                    P.tt("dve", lbT.v, lbT.v, eT[:, :, l2, :], ALU.add)
                P.tt("dve", lbT.v, lbT.v, tT.v, ALU.mult)
            P.ts("dve", omlF.v, lbF.v, -1.0, ALU.mult, 1.0, ALU.add)
            P.ts("dve", omlT.v, lbT.v, -1.0, ALU.mult, 1.0, ALU.add)

            fT_ = P.sb([128, 34, 128], name="f"); logf = P.sb([128, 34, 128], name="logf"); kk = P.sb([128, 34, 128], name="kk")
            kkT = P.sb([128, TB], name="kkT"); qT = P.sb([128, TB], name="qT"); vb = P.sb([128, 34, 128], name="vb")
            oacc = P.sb([128, TB], name="oacc")
            Sst = [P.sb([128, 128], name="S%d" % i) for i in range(2)]
            tmp = {n: [P.sb([128, 128], name="%s%d" % (n, i)) for i in range(2)]
                   for n in ("emc", "khat", "ecum", "qhat", "e1", "e2", "qtil", "ktil", "atm", "cl", "qtc", "ktc", "atc")}
            etot = [P.sb([128, 2], name="etot%d" % i) for i in range(2)]
            pA = [P.ps([128, 512], name="pA%d" % i) for i in range(2)]
            pB = [P.ps([128, 512], name="pB%d" % i) for i in range(2)]
            pC = [P.ps([128, 512], name="pC%d" % i) for i in range(2)]
            pU = [P.ps([128, 512], name="pU%d" % i) for i in range(2)]
            for b in range(2):
                for q in range(2):
                    h0 = q * (TB // 2)
                    P.dma("sp", qT[:, h0:h0 + TB // 2], S["hq"][:, b * TB + h0:b * TB + h0 + TB // 2])
                P.dma("sp", vb.v, S["hv"][b * TB:(b + 1) * TB, :].r("(t p) d -> p t d", p=128))
                for d in range(2):
                    zn, znF = ("zf", "zfF") if d == 0 else ("zb", "zbF")
                    LT = hc[:, 6 * d + 0, :]; BmL = hc[:, 6 * d + 1, :]; Dw = hc[:, 6 * d + 2, :]
                    Dc = hc[:, 6 * d + 3, :]; Mw = hc[:, 6 * d + 4, :]; Mc = hc[:, 6 * d + 5, :]
                    P.dma("sp", fT_.v, S[zn][b * TB:(b + 1) * TB, :].r("(t p) d -> p t d", p=128))
                    for q in range(2):
                        h0 = q * (TB // 2)
                        P.dma("sp", kkT[:, h0:h0 + TB // 2], S[znF][:, b * TB + h0:b * TB + h0 + TB // 2])
                    P.act(fT_.v, fT_.v, AF.Sigmoid)
                    P.tt("dve", fT_.v, fT_.v, omlT[:, d, :].unsq(1).bc([128, 34, 128]), ALU.mult)
                    P.tt("dve", fT_.v, fT_.v, lbT[:, d, :].unsq(1).bc([128, 34, 128]), ALU.add)
                    P.ts("dve", fT_.v, fT_.v, 1e-20, ALU.max)
                    P.act(logf.v, fT_.v, AF.Ln)
                    P.ts("pool", kk.v, fT_.v, -1.0, ALU.mult, 1.0, ALU.add)
                    P.act(kkT.v, kkT.v, AF.Sigmoid)
                    P.ts("dve", kkT.v, kkT.v, omlF[:, d:d + 1], ALU.mult, lbF[:, d:d + 1], ALU.add)
                    P.ts("dve", kkT.v, kkT.v, -1.0, ALU.mult, 1.0, ALU.add)
                    si = 0
                    P.memset("dve", Sst[0].v, 0.0)
                    if d == 0:
                        order = list(range(34)); corder = (0, 1)
                    else:
                        order = [1, 0] + list(range(33, 1, -1)); corder = (1, 0)
                    for n_, ti in enumerate(order):
                        i2 = n_ % 2
                        c0 = ti * 128
                        lf = logf[:, ti, :]
                        T_ = {k: v[i2] for k, v in tmp.items()}
                        P.mm(pA[i2][:, 0:128], BmL, lf)
                        P.act(T_["emc"].v, pA[i2][:, 0:128], AF.Exp)
                        P.tt("pool", T_["khat"].v, kk[:, ti, :], T_["emc"].v, ALU.mult)
                        P.mm(pB[i2][:, 0:128], lf, LT)
                        P.act(T_["ecum"].v, pB[i2][:, 0:128], AF.Exp)
                        P.tt("dve", T_["qhat"].v, qT[:, c0:c0 + 128], T_["ecum"].v, ALU.mult)
                        P.mm(pC[i2][:, 0:128], lf, Dw)
                        P.ts("dve", T_["cl"].v, pC[i2][:, 0:128], -40.0, ALU.max, 40.0, ALU.min)
                        P.act(T_["e1"].v, T_["cl"].v, AF.Exp)
                        P.act(T_["e2"].v, T_["cl"].v, AF.Exp, scale=-1.0)
                        P.tt("dve", T_["qtil"].v, qT[:, c0:c0 + 128], T_["e1"].v, ALU.mult)
                        P.tt("pool", T_["ktil"].v, kkT[:, c0:c0 + 128], T_["e2"].v, ALU.mult)
                        P.mm(pA[i2][:, 0:128], T_["ktil"].v, T_["qtil"].v)
                        P.tt("dve", T_["atm"].v, pA[i2][:, 0:128], Mw, ALU.mult)
                        P.mm(pC[i2][:, 128:256], lf, Dc)
                        P.ts("dve", T_["cl"].v, pC[i2][:, 128:256], 0.0, ALU.min)
                        P.act(T_["e1"].v, T_["cl"].v, AF.Exp)
                        P.ts("dve", T_["cl"].v, pC[i2][:, 128:256], 0.0, ALU.max)
                        P.act(T_["e2"].v, T_["cl"].v, AF.Exp, scale=-1.0)
                        P.tt("dve", T_["qtc"].v, qT[:, c0:c0 + 128], T_["e1"].v, ALU.mult)
                        P.tt("pool", T_["ktc"].v, kkT[:, c0:c0 + 128], T_["e2"].v, ALU.mult)
                        P.mm(pA[i2][:, 128:256], T_["ktc"].v, T_["qtc"].v)
                        P.tt("dve", T_["atc"].v, pA[i2][:, 128:256], Mc, ALU.mult)
                        P.tt("pool", T_["atm"].v, T_["atm"].v, T_["atc"].v, ALU.add)
                        P.mm(pB[i2][:, 0:2], lf, cib.v)
                        P.act(etot[i2].v, pB[i2][:, 0:2], AF.Exp)
                        P.mm(pU[0][:, 0:128], T_["khat"][0:64, :], vb[0:64, ti, :])
                        P.mm(pU[1][:, 0:128], T_["khat"][64:128, :], vb[64:128, ti, :])
                        po = pC[i2]
                        P.mm(po[:, 0:128], vb[:, ti, :], T_["atm"].v, start=True, stop=False)
                        for n2, cc in enumerate(corder):
                            Scur = Sst[si % 2]; Snew = Sst[(si + 1) % 2]; si += 1
                            P.mm(po[:, cc * 64:(cc + 1) * 64], Scur.v, T_["qhat"][:, cc * 64:(cc + 1) * 64],
                                 start=False, stop=(n2 == 1))
                            P.stt(Snew.v, Scur.v, etot[i2][:, cc:cc + 1], pU[cc][:, 0:128], ALU.mult, ALU.add)
                        if d == 0:
                            P.cp("act", oacc[:, c0:c0 + 128], po[:, 0:128])
                        else:
                            P.tt("dve", oacc[:, c0:c0 + 128], oacc[:, c0:c0 + 128], po[:, 0:128], ALU.add)
                gT = kkT
                for q in range(2):
                    h0 = q * (TB // 2)
                    P.dma("sp", gT[:, h0:h0 + TB // 2], S["hg"][:, b * TB + h0:b * TB + h0 + TB // 2])
                P.act(gT.v, gT.v, AF.Silu)
                sq = logf.v.r("p a b -> p (a b)")
                rs = kk.v.r("p a b -> p (a b)")
                P.tt("pool", sq, oacc.v, oacc.v, ALU.mult)
                for gi, (c0, n) in enumerate(col_groups(TB)):
                    pm = pA[gi % 2]
                    P.mm(pm[:, :n], C["ones"].v, sq[:, c0:c0 + n])
                    P.ts("dve", rs[:, c0:c0 + n], pm[:, :n], 1.0 / 128, ALU.mult, EPS, ALU.add)
                P.act(rs, rs, AF.Sqrt)
                P.recip(rs, rs)
                P.stt(oacc.v, oacc.v, gon[:, 0:1], rs, ALU.mult, ALU.mult)
                P.tt("dve", oacc.v, oacc.v, gT.v, ALU.mult)
                self.store_y(2, b, oacc.v)

    def store_y(self, k, b, src, q0=0, qn=TB):
        P, S = self.P, self.S
        for j in range(8):
            for (ts_, n, loc) in ((256 + j * 512, 512, b * 512), (j * 32, 32, 1024 + b * 32)):
                lo = max(ts_, q0); hi = min(ts_ + n, q0 + qn)
                if lo >= hi:
                    continue
                P.dma("sp", S["y_loc"][j * 384 + k * 128:j * 384 + (k + 1) * 128, loc + lo - ts_:loc + hi - ts_],
                      src[:, lo - q0:hi - q0])

    def emit_yx(self):
        self.tap("y_loc", self.S["y_loc"])
        self.P.allgather(self.S["y_loc"], self.S["y_all"])
        P, S = self.P, self.S
        with P.scope():
            bb = [P.sb([128, 3, NT], name="yb%d" % i) for i in range(2)]
            for r in range(8):
                sl = S["y_all"].t[r * 3072:(r + 1) * 3072, :][bass.ds(self.off384, 384), :].rearrange("(k p) t -> p k t", p=128)
                b_ = bb[r % 2]
                P.dma("pool", b_.v, View(S["y_all"], sl))
                P.dma("sp", S["y_in"][r * 384:(r + 1) * 384, :].r("(k p) t -> p k t", p=128), b_.v)
        self.tap("y_in", S["y_in"])


class MK4(MK3):
    def emit_t2_merge(self, l, W):
        P, I, S, C = self.P, self.I, self.S, self.C
        with P.scope():
            yT = P.sb([128, 8, 3, 512], name="yT")
            wb = [P.sb([128, 24, 512], name="wbr%d" % i) for i in range(2)]
            gt = [P.sb([128, 512], name="gt%d" % i) for i in range(3)]
            acc = [P.sb([128, 512], name="acc%d" % i) for i in range(2)]
            tmp = [P.sb([128, 512], name="tmp%d" % i) for i in range(2)]
            pk = [P.ps([128, 512], name="pk%d" % i) for i in range(4)]
            wsrc = W["wbr"].v.r("(kr p) c -> p kr c", p=128)
            ig = 0; ia = 0; ip = 0; iw = 0
            for (t0, tn) in TG:
                for r in range(8):
                    P.dma("sp", yT[:, r, :, :tn], S["y_in"][r * 384:(r + 1) * 384, t0:t0 + tn].r("(k p) t -> p k t", p=128))
                for og in range(4):
                    w_ = wb[iw % 2]; iw += 1
                    for q in range(3):
                        P.dma("sp", w_[:, q * 8:(q + 1) * 8, :], wsrc[:, q * 8:(q + 1) * 8, og * 512:(og + 1) * 512])
                    for c4 in range(4):
                        oc = og * 4 + c4
                        a_ = acc[ia % 2]; ia += 1
                        for k in range(3):
                            p_ = pk[ip % 4]; ip += 1
                            for r in range(8):
                                P.mm(p_[:, :tn], w_[:, k * 8 + r, c4 * 128:(c4 + 1) * 128], yT[:, r, k, :tn],
                                     start=(r == 0), stop=(r == 7))
                            g_ = gt[ig % 3]; ig += 1
                            ch = k * 16 + oc
                            P.dma("act", g_[:, :tn], S["gatesT"][ch * 128:(ch + 1) * 128, t0:t0 + tn])
                            if k == 0:
                                P.tt("dve", a_[:, :tn], p_[:, :tn], g_[:, :tn], ALU.mult)
                            else:
                                t_ = tmp[k % 2]
                                P.tt("dve", t_[:, :tn], p_[:, :tn], g_[:, :tn], ALU.mult)
                                P.tt("pool", a_[:, :tn], a_[:, :tn], t_[:, :tn], ALU.add)
                        P.dma("sp", S["mT"][oc * 128:(oc + 1) * 128, t0:t0 + tn], a_[:, :tn])
        self.tap("mT", S["mT"])

    def emit_t2_wout(self, l, W):
        P, I, S, C = self.P, self.I, self.S, self.C
        with P.scope():
            mT = P.sb([128, 16, NT], name="mT")
            for q in range(4):
                P.dma("sp", mT[:, q * 4:(q + 1) * 4, :], S["mT"].v.r("(k p) t -> p k t", p=128)[:, q * 4:(q + 1) * 4, :])
            M2 = P.sb([128, D], name="M2"); A = P.sb([128, D], name="A"); B = P.sb([128, D], name="B")
            G = P.sb([128, D], name="G")
            P.dma("act", G.v, I["gffn"][self.lidx[l]].partition_broadcast(128))
            wo = [P.sb([128, 16, 512], name="wo%d" % i) for i in range(2)]
            xt = P.sb([128, D], name="xt"); h = P.sb([128, D], name="h"); ss = P.sb([128, 1], name="ss")
            tmp = [P.sb([128, 512], name="tmp%d" % i) for i in range(2)]
            hTt = [P.sb([128, 16, 128], name="hTt%d" % i) for i in range(2)]
            pm = [P.ps([128, 512], name="pm%d" % i) for i in range(4)]
            pst = [P.ps([128, 512], name="pst%d" % i) for i in range(2)]
            wsrc = W["wout"].v.r("(k p) c -> p k c", p=128)
            iw = 0; ip = 0; ntr = 0
            curk = -1
            for ti, (t0, rows) in enumerate(tiles_of(NT)):
                kind = kind_of_tile(t0)
                if kind != curk:
                    curk = kind
                    self.load_modvec(M2.v, l, kind, 2)
                    self.load_modvec(A.v, l, kind, 4)
                    self.load_modvec(B.v, l, kind, 3)
                    P.stt(A.v, A.v, 1.0, G.v, ALU.add, ALU.mult)
                P.dma("sp", xt[:rows, :], S["xres"][t0:t0 + rows, :])
                for n in range(4):
                    w_ = wo[iw % 2]; iw += 1
                    for q in range(4):
                        P.dma("sp", w_[:, q * 4:(q + 1) * 4, :], wsrc[:, q * 4:(q + 1) * 4, n * 512:(n + 1) * 512])
                    p_ = pm[ip % 4]; ip += 1
                    for k in range(16):
                        P.mm(p_[:rows, :], mT[:, k, t0:t0 + rows], w_[:, k, :], start=(k == 0), stop=(k == 15))
                    t_ = tmp[n % 2]
                    P.tt("dve", t_[:rows, :], p_[:rows, :], M2[:rows, n * 512:(n + 1) * 512], ALU.mult)
                    P.tt("pool", xt[:rows, n * 512:(n + 1) * 512], xt[:rows, n * 512:(n + 1) * 512], t_[:rows, :], ALU.add)
                P.dma("sp", S["xres"][t0:t0 + rows, :], xt[:rows, :])
                self.norm_rows(xt[:rows, :], rows, h[:rows, :], ss[:rows, :], A[:rows, :], B[:rows, :])
                hT = hTt[ti % 2]
                for q in range(4):
                    pt = pst[ntr % 2]; ntr += 1
                    for kk in range(4):
                        k = q * 4 + kk
                        P.tr(pt[:, kk * 128:kk * 128 + rows], h[:rows, k * 128:(k + 1) * 128], C["ident"][:rows, :rows])
                    P.cp("act" if q % 2 == 0 else "dve", hT[:, q * 4:(q + 1) * 4, :rows],
                         pt.v.r("p (a b) -> p a b", a=4)[:, :, :rows])
                P.dma("sp", S["h2T"].v.r("(k p) t -> p k t", p=128)[:, :, t0:t0 + rows], hT[:, :, :rows])
        self.tap("xres_mid", S["xres"]); self.tap("h2T", S["h2T"])

    def emit_t2_scores(self, l, W):
        P, I, S, C = self.P, self.I, self.S, self.C
        with P.scope():
            h2T = P.sb([128, 16, NT], name="h2T")
            for q in range(4):
                P.dma("sp", h2T[:, q * 4:(q + 1) * 4, :], S["h2T"].v.r("(k p) t -> p k t", p=128)[:, q * 4:(q + 1) * 4, :])
            skT = P.sb([128, 2, 128], name="skT")
            P.dma("act", skT.v, I["skT"][self.lidx[l]].rearrange("j d n -> d j n"))
            wq = [P.sb([128, 16, 512], name="wq%d" % i) for i in range(2)]
            qs = P.sb([128, D], name="qs"); qT = P.sb([128, 16, 128], name="qT"); sc = P.sb([128, D], name="sc")
            pm = [P.ps([128, 512], name="pm%d" % i) for i in range(4)]
            pst = [P.ps([128, 512], name="pst%d" % i) for i in range(2)]
            wsrc = W["wq"].v.r("(k p) c -> p k c", p=128)
            iw = 0; ip = 0; ntr = 0
            for ti, (t0, rows) in enumerate(tiles_of(NT)):
                for n in range(4):
                    w_ = wq[iw % 2]; iw += 1
                    for q in range(4):
                        P.dma("sp", w_[:, q * 4:(q + 1) * 4, :], wsrc[:, q * 4:(q + 1) * 4, n * 512:(n + 1) * 512])
                    p_ = pm[ip % 4]; ip += 1
                    for k in range(16):
                        P.mm(p_[:rows, :], h2T[:, k, t0:t0 + rows], w_[:, k, :], start=(k == 0), stop=(k == 15))
                    P.cp("act" if n % 2 == 0 else "dve", qs[:rows, n * 512:(n + 1) * 512], p_[:rows, :])
                for q in range(4):
                    pt = pst[ntr % 2]; ntr += 1
                    for kk in range(4):
                        k = q * 4 + kk
                        P.tr(pt[:, kk * 128:kk * 128 + rows], qs[:rows, k * 128:(k + 1) * 128], C["ident"][:rows, :rows])
                    P.cp("act" if q % 2 == 0 else "dve", qT[:, q * 4:(q + 1) * 4, :rows],
                         pt.v.r("p (a b) -> p a b", a=4)[:, :, :rows])
                for n in range(4):
                    p_ = pm[ip % 4]; ip += 1
                    for kk in range(4):
                        hj = n * 4 + kk
                        P.mm(p_[:rows, kk * 128:(kk + 1) * 128], qT[:, hj, :rows], skT[:, hj % 2, :])
                    P.cp("act" if n % 2 == 0 else "dve", sc[:rows, n * 512:(n + 1) * 512], p_[:rows, :])
                P.dma("sp", S["scores"][t0:t0 + rows, :], sc[:rows, :])
        self.tap("scores", S["scores"])

    def emit_t2_topk(self, l):
        P, I, S, C = self.P, self.I, self.S, self.C
        NEG = -1.0e30
        with P.scope():
            scb = [P.sb([128, 16, 128], name="sc%d" % i) for i in range(2)]
            scr = P.sb([128, 128], name="scr"); top = P.sb([128, 16, 16], name="top")
            cand = P.sb([128, 8, 256], name="cand"); scr2 = P.sb([128, 256], name="scr2")
            c16 = P.sb([128, 8, 16], name="c16"); ex16 = P.sb([128, 8, 16], name="ex16")
            negm = P.sb([128, 8], name="negm"); Z = P.sb([128, 8], name="Z"); nb = P.sb([128, 8], name="nb")
            tau = P.sb([128, 8], name="tau")
            Sq = [P.sb([128, 16, 128], name="Sq%d" % i) for i in range(2)]
            Ex = [P.sb([128, 16, 128], name="Ex%d" % i) for i in range(2)]
            gacc = [P.sb([128, 16, 128], name="gacc%d" % i) for i in range(2)]
            gts = [P.sb([128, 16, 128], name="gts%d" % i) for i in range(2)]
            pst = [P.ps([128, 512], name="pst%d" % i) for i in range(4)]
            ish = 0; ig = 0; ntr = 0
            for ti, (t0, rows) in enumerate(tiles_of(NT)):
                sc = scb[ti % 2]
                P.dma("sp", sc[:rows].r("p a b -> p (a b)"), S["scores"][t0:t0 + rows, :])
                R = slice(0, rows)
                for hj in range(16):
                    P.op("dve", lambda E: E.max(out=top[R, hj, 0:8].ap, in_=sc[R, hj, :].ap), reads=[sc.v], writes=[top.v])
                    P.op("dve", lambda E: E.match_replace(out=scr[R, :].ap, in_to_replace=top[R, hj, 0:8].ap,
                                                          in_values=sc[R, hj, :].ap, imm_value=NEG),
                         reads=[sc.v, top.v], writes=[scr.v])
                    P.op("dve", lambda E: E.max(out=top[R, hj, 8:16].ap, in_=scr[R, :].ap), reads=[scr.v], writes=[top.v])
                tv = top.v.r("p (h j) a -> p h j a", j=2)
                P.tt("dve", cand[R].r("p h (a b) -> p h a b", a=16),
                     tv[R, :, 0, :].unsq(3).bc([rows, 8, 16, 16]), tv[R, :, 1, :].unsq(2).bc([rows, 8, 16, 16]), ALU.add)
                for h in range(8):
                    P.op("dve", lambda E: E.max(out=c16[R, h, 0:8].ap, in_=cand[R, h, :].ap), reads=[cand.v], writes=[c16.v])
                    P.op("dve", lambda E: E.match_replace(out=scr2[R, :].ap, in_to_replace=c16[R, h, 0:8].ap,
                                                          in_values=cand[R, h, :].ap, imm_value=NEG),
                         reads=[cand.v, c16.v], writes=[scr2.v])
                    P.op("dve", lambda E: E.max(out=c16[R, h, 8:16].ap, in_=scr2[R, :].ap), reads=[scr2.v], writes=[c16.v])
                P.ts("dve", negm[R], c16[R, :, 0], -1.0, ALU.mult)
                P.cp("dve", tau[R], c16[R, :, 15])
                for h in range(8):
                    P.act(ex16[R, h, :], c16[R, h, :], AF.Exp, bias=negm[R, h:h + 1], accum=Z[R, h:h + 1])
                P.act(Z[R], Z[R], AF.Ln)
                P.tt("dve", nb[R], negm[R], Z[R], ALU.subtract)
                for iq in range(8):
                    ga = gacc[ig % 2]; ig += 1
                    for h in range(8):
                        s_ = Sq[ish % 2]; e_ = Ex[ish % 2]; ish += 1
                        P.tt("dve", s_[R], sc[R, 2 * h, iq * 16:(iq + 1) * 16].unsq(2).bc([rows, 16, 128]),
                             sc[R, 2 * h + 1, :].unsq(1).bc([rows, 16, 128]), ALU.add)
                        P.act(e_[R], s_[R], AF.Exp, bias=nb[R, h:h + 1])
                        if h == 0:
                            P.stt(ga[R], s_[R], tau[R, h:h + 1], e_[R], ALU.is_ge, ALU.mult)
                        else:
                            P.stt(e_[R], s_[R], tau[R, h:h + 1], e_[R], ALU.is_ge, ALU.mult)
                            P.tt("pool", ga[R], ga[R], e_[R], ALU.add)
                    g_ = gts[ig % 2]
                    for q in range(4):
                        pt = pst[ntr % 4]; ntr += 1
                        for kk in range(4):
                            i = q * 4 + kk
                            P.tr(pt[:, kk * 128:kk * 128 + rows], ga[R, i, :], C["ident"][:rows, :rows])
                        P.cp("act", g_[:, q * 4:(q + 1) * 4, :rows], pt.v.r("p (a b) -> p a b", a=4)[:, :, :rows])
                    P.dma("sp", S["GT"][iq * 2048:(iq + 1) * 2048, t0:t0 + rows].r("(i j) t -> j i t", j=128), g_[:, :, :rows])
        self.tap("GT", S["GT"])

    def emit_t2_pass1(self, l, W):
        P, I, S, C = self.P, self.I, self.S, self.C
        with P.scope():
            h2T = P.sb([128, 16, NT], name="h2T")
            for q in range(4):
                P.dma("sp", h2T[:, q * 4:(q + 1) * 4, :], S["h2T"].v.r("(k p) t -> p k t", p=128)[:, q * 4:(q + 1) * 4, :])
            ut = [P.sb([128, 16, 512], name="ut%d" % i) for i in range(2)]
            a = [P.sb([128, 512], name="a%d" % i) for i in range(3)]
            g = [P.sb([128, 512], name="g%d" % i) for i in range(3)]
            pm = [P.ps([128, 512], name="pm%d" % i) for i in range(4)]
            usrc = W["ut"].v.r("(k p) n -> p k n", p=128)
            ip = 0; ia = 0
            for gi in range(32):
                u_ = ut[gi % 2]
                for q in range(4):
                    P.dma("sp", u_[:, q * 4:(q + 1) * 4, :], usrc[:, q * 4:(q + 1) * 4, gi * 512:(gi + 1) * 512])
                for c4 in range(4):
                    n0 = gi * 512 + c4 * 128
                    for (t0, tn) in TG:
                        p_ = pm[ip % 4]; ip += 1
                        a_ = a[ia % 3]; g_ = g[ia % 3]; ia += 1
                        P.dma("act", g_[:, :tn], S["GT"][n0:n0 + 128, t0:t0 + tn])
                        for k in range(16):
                            P.mm(p_[:, :tn], u_[:, k, c4 * 128:(c4 + 1) * 128], h2T[:, k, t0:t0 + tn], start=(k == 0), stop=(k == 15))
                        P.act(a_[:, :tn], p_[:, :tn], AF.Gelu_apprx_tanh)
                        P.tt("dve" if ia % 2 else "pool", a_[:, :tn], a_[:, :tn], g_[:, :tn], ALU.mult)
                        P.dma("sp", S["WT"][n0:n0 + 128, t0:t0 + tn], a_[:, :tn])
        self.tap("WT", S["WT"])

    def emit_t2_pass2(self, l, W, last):
        P, I, S, C = self.P, self.I, self.S, self.C
        with P.scope():
            yacc = P.sb([128, 9, D], name="yacc")
            vg = [P.sb([128, 4, D], name="vg%d" % i) for i in range(2)]
            wt = [P.sb([128, 4, NT], name="wt%d" % i) for i in range(2)]
            pm = [P.ps([128, 512], name="pm%d" % i) for i in range(4)]
            ip = 0
            for gi in range(32):
                v_ = vg[gi % 2]; w_ = wt[gi % 2]
                P.dma("sp", v_.v, W["vv"][gi * 512:(gi + 1) * 512, :].r("(c p) d -> p c d", p=128))
                P.dma("act", w_.v, S["WT"][gi * 512:(gi + 1) * 512, :].r("(c p) t -> p c t", p=128))
                for ti, (t0, rows) in enumerate(tiles_of(NT)):
                    for dg in range(4):
                        p_ = pm[ip % 4]; ip += 1
                        for c in range(4):
                            P.mm(p_[:rows, :], w_[:, c, t0:t0 + rows], v_[:, c, dg * 512:(dg + 1) * 512], start=(c == 0), stop=(c == 3))
                        dst = yacc[:rows, ti, dg * 512:(dg + 1) * 512]
                        if gi == 0:
                            P.cp("act", dst, p_[:rows, :])
                        else:
                            P.tt("dve", dst, dst, p_[:rows, :], ALU.add)
            M5 = P.sb([128, D], name="M5")
            xt = P.sb([128, D], name="xt")
            ss = P.sb([128, 1], name="ss")
            curk = -1
            if last:
                Gf = vg[0].v.r("p c d -> p (c d)")[:, 0:D]
                P.dma("act", Gf, I["gfin"].partition_broadcast(128))
                hh = vg[1].v.r("p c d -> p (c d)")[:, 0:D]
            for ti, (t0, rows) in enumerate(tiles_of(NT)):
                kind = kind_of_tile(t0)
                if kind != curk:
                    curk = kind
                    self.load_modvec(M5.v, l, kind, 5)
                P.dma("sp", xt[:rows, :], S["xres"][t0:t0 + rows, :])
                P.tt("dve", yacc[:rows, ti, :], yacc[:rows, ti, :], M5[:rows, :], ALU.mult)
                P.tt("dve", xt[:rows, :], xt[:rows, :], yacc[:rows, ti, :], ALU.add)
                P.dma("sp", S["xres"][t0:t0 + rows, :], xt[:rows, :])
                if last and t0 < 1024:
                    self.norm_rows(xt[:rows, :], rows, hh[:rows, :], ss[:rows, :], Gf[:rows, :], None)
                    P.dma("sp", self.out[t0:t0 + rows, :], hh[:rows, :])
        self.tap("xres_end", S["xres"])

    def build(self):
        P, I, S = self.P, self.I, self.S
        st = self.stages
        def on(name):
            return st is None or name in st
        if on("init"):
            for q in range(4):
                P.dma("pool", S["xres"][q * 272:(q + 1) * 272, :], I["x"][q * 272:(q + 1) * 272, :])
        if on("lmod"):
            self.emit_lmod()
        Wn = self.gather_weights(self.layers[0]) if on("gw") else None
        for idx, l in enumerate(self.layers):
            W = Wn
            if on("gw") and idx + 1 < len(self.layers):
                Wn = self.gather_weights(self.layers[idx + 1])
            if on("t1a"): self.emit_t1a(l)
            if on("t1b"): self.emit_t1b(l, W)
            if on("t1c"): self.emit_t1c(l)
            if on("rg"): self.emit_rg(l)
            if on("da"): self.emit_da(l)
            if on("hg"): self.emit_hg(l)
            if on("yx"):
                self.emit_yx()
            elif "y_loc" in self.dbg:
                self.tap("y_loc", S["y_loc"])
            if on("merge"): self.emit_t2_merge(l, W)
            if on("wout"): self.emit_t2_wout(l, W)
            if on("scores"): self.emit_t2_scores(l, W)
            if on("topk"): self.emit_t2_topk(l)
            if on("pass1"): self.emit_t2_pass1(l, W)
            if on("pass2"): self.emit_t2_pass2(l, W, l == 3)
        return P.finish()


OFF = {"rgx": 0, "rgg": 1024, "q": 2048, "k": 3072, "v": 4096, "hq": 5120, "zf": 6144, "zb": 7168, "hv": 8192, "hg": 9216}


def rope_tabs():
    S = 4096
    rows = S // 64
    r = np.repeat(np.arange(rows, dtype=np.float32), 64)
    col = np.tile(np.arange(64, dtype=np.float32), rows)
    inv = (np.float32(10000.0) ** (-np.arange(0, 32, 2, dtype=np.float32) / np.float32(32))).astype(np.float32)
    ang = np.concatenate([r[:, None] * inv, col[:, None] * inv], axis=-1).astype(np.float32)
    return np.cos(ang).astype(np.float32), np.sin(ang).astype(np.float32)


def hg_consts():
    s = np.arange(128)[:, None]
    t = np.arange(128)[None, :]
    same = (s // 64) == (t // 64)
    sameh = (s // 32) == (t // 32)
    out = []
    for d in range(2):
        if d == 0:
            LT = same & (s <= t)
            midw = (t // 32) * 32 + 15
            bnd = (t // 64) * 64 + 31
            Lmid = same & (s <= midw)
            Lb = same & (s <= bnd)
            Mw = sameh & (s <= t)
            Mc = same & ((s % 64) < 32) & ((t % 64) >= 32)
        else:
            LT = same & (s >= t)
            midw = (t // 32) * 32 + 16
            bnd = (t // 64) * 64 + 32
            Lmid = same & (s >= midw)
            Lb = same & (s >= bnd)
            Mw = sameh & (s >= t)
            Mc = same & ((s % 64) >= 32) & ((t % 64) < 32)
        f = lambda a: a.astype(np.float32)
        out += [f(LT), f(same) - f(LT), f(LT) - f(Lmid), f(LT) - f(Lb), f(Mw), f(Mc)]
    ci = np.zeros((128, 2), np.float32)
    ci[:64, 0] = 1
    ci[64:, 1] = 1
    return np.stack(out).astype(np.float32), ci


def wc_cols(c):
    cols = []
    for g in GRP:
        if g in ("qsw", "ksw"):
            base = OFF[g[0]] + c * 128
            idx = []
            for j in range(2):
                idx += list(range(base + j * 64 + 32, base + j * 64 + 64)) + list(range(base + j * 64, base + j * 64 + 32))
            cols += idx
        else:
            base = OFF[g] + c * 128
            cols += list(range(base, base + 128))
    return np.array(cols)


def pack_inputs(inp, layers=(0, 1, 2, 3), names=None):
    L = list(layers)
    cosT, sinT = rope_tabs()
    hgc, ci = hg_consts()
    pidx = np.arange(128) % 64
    cosF = np.ascontiguousarray(cosT[:, pidx % 32].T)
    sgn = np.where(pidx < 32, -1.0, 1.0).astype(np.float32)
    sinF = np.ascontiguousarray((sinT[:, pidx % 32] * sgn[None, :]).T)
    cv = np.stack([inp["c"][0], inp["c"][1], inp["c_ctx"]])
    cvT = np.ascontiguousarray(cv.T.reshape(16, 128, 3).transpose(1, 0, 2))
    outs = []

    def want(n):
        return names is None or n in names
    for r in range(8):
        d = {}
        if want("x"):
            d["x"] = np.concatenate([inp["x"][0, r * 512:(r + 1) * 512], inp["x"][1, r * 512:(r + 1) * 512],
                                     inp["ctx"][0, r * 32:(r + 1) * 32], inp["ctx"][1, r * 32:(r + 1) * 32]], axis=0)
        if want("cvT"): d["cvT"] = cvT
        if want("wada"): d["wada"] = np.ascontiguousarray(inp["w_ada"][L][:, :, r * 1536:(r + 1) * 1536])
        if want("bada"): d["bada"] = np.ascontiguousarray(inp["b_ada"][L][:, r * 1536:(r + 1) * 1536])
        if want("gmix"): d["gmix"] = np.ascontiguousarray(inp["norm_mix_g"][L])
        if want("gffn"): d["gffn"] = np.ascontiguousarray(inp["norm_ffn_g"][L])
        if want("gfin"): d["gfin"] = inp["final_norm_g"]
        if want("wc"):
            cols = wc_cols(r)
            d["wc"] = np.stack([inp["w_in"][l][:, cols] for l in L])
        if want("wg"): d["wg"] = np.stack([inp["w_in"][l][r * 256:(r + 1) * 256, 10240:] for l in L])
        if want("bgT"):
            d["bgT"] = np.ascontiguousarray(inp["b_gate"][L].reshape(len(L), 48, 128).transpose(2, 0, 1))
        sl = slice(r * 128, (r + 1) * 128)
        if want("rgcw"): d["rgcw"] = np.ascontiguousarray(inp["rg_conv_w"][L][:, :, sl].transpose(2, 0, 1))
        if want("rgcb"): d["rgcb"] = np.ascontiguousarray(inp["rg_conv_b"][L][:, sl].T)
        if want("rgwa"): d["rgwa"] = np.ascontiguousarray(inp["rg_wa"][L][:, :, r])
        if want("rgwx"): d["rgwx"] = np.ascontiguousarray(inp["rg_wx"][L][:, :, r])
        if want("rgba"): d["rgba"] = np.ascontiguousarray(inp["rg_ba"][L][:, :, sl].transpose(2, 0, 1))
        if want("rgbx"): d["rgbx"] = np.ascontiguousarray(inp["rg_bx"][L][:, :, sl].transpose(2, 0, 1))
        if want("rglam"): d["rglam"] = np.ascontiguousarray(inp["rg_lambda"][L][:, :, sl].transpose(2, 0, 1))
        if want("dalq"): d["dalq"] = np.ascontiguousarray(inp["da_lq"][L].reshape(len(L), 128))
        if want("dalk"): d["dalk"] = np.ascontiguousarray(inp["da_lk"][L].reshape(len(L), 128))
        if want("dasg"): d["dasg"] = np.ascontiguousarray(inp["da_subln_g"][L].T)
        if want("hglbF"): d["hglbF"] = np.ascontiguousarray(inp["hg_lb"][:, :, sl].transpose(2, 0, 1))
        if want("hglbT"): d["hglbT"] = np.ascontiguousarray(inp["hg_lb"][:, :, sl])
        if want("hgon"): d["hgon"] = np.ascontiguousarray(inp["hg_onorm_g"][L].T)
        if want("cosF"): d["cosF"] = cosF
        if want("sinF"): d["sinF"] = sinF
        if want("wbr"): d["wbr"] = np.stack([inp["w_branch"][l].reshape(3072, 2048)[r * 384:(r + 1) * 384] for l in L])
        if want("wout"): d["wout"] = np.stack([inp["w_out"][l][r * 256:(r + 1) * 256] for l in L])
        if want("wq"): d["wq"] = np.stack([inp["peer_wq"][l][r * 256:(r + 1) * 256] for l in L])
        if want("ut"): d["ut"] = np.stack([np.ascontiguousarray(inp["peer_u"][l][:, r * 256:(r + 1) * 256].T) for l in L])
        if want("vv"): d["vv"] = np.stack([inp["peer_v"][l][r * 2048:(r + 1) * 2048] for l in L])
        if want("skT"): d["skT"] = np.ascontiguousarray(inp["peer_subkeys"][L].transpose(0, 1, 3, 2))
        if want("ident"): d["ident"] = np.eye(128, dtype=np.float32)
        if want("hgc"): d["hgc"] = hgc
        if want("ci"): d["ci"] = ci
        outs.append({k: np.ascontiguousarray(v, dtype=np.float32) for k, v in d.items()})
    return outs


def unpack_output(res):
    out = np.zeros((2, 4096, 2048), np.float32)
    for r in range(8):
        o = res[r]["out"]
        out[0, r * 512:(r + 1) * 512] = o[:512]
        out[1, r * 512:(r + 1) * 512] = o[512:1024]
    return out


from concourse.bass_utils import run_bass_kernel_spmd

_CACHE = {}


def kernel(**inputs):
    inp = {k: np.asarray(v) for k, v in inputs.items()}
    if "nc" not in _CACHE:
        m = MK4(layers=(0, 1, 2, 3), stages=None, dbg=())
        _CACHE["nc"] = m.build()
        _CACHE["names"] = set(m.I.keys())
    ims = pack_inputs(inp, layers=(0, 1, 2, 3), names=_CACHE["names"])
    res = run_bass_kernel_spmd(_CACHE["nc"], ims, core_ids=list(range(8)))
    return unpack_output(res.results)
```

```python
import contextlib
import numpy as np
import concourse.bass as bass
import concourse.mybir as mybir

F32 = mybir.dt.float32
BF16 = mybir.dt.bfloat16
AF = mybir.ActivationFunctionType
ALU = mybir.AluOpType
AX = mybir.AxisListType


class View:
    __slots__ = ("buf", "ap")

    def __init__(self, buf, ap):
        self.buf = buf
        self.ap = ap

    def __getitem__(self, k):
        return View(self.buf, self.ap[k])

    def r(self, pat, **kw):
        return View(self.buf, self.ap.rearrange(pat, **kw))

    def bc(self, shape):
        return View(self.buf, self.ap.to_broadcast(list(shape)))

    def unsq(self, ax):
        return View(self.buf, self.ap.unsqueeze(ax))

    def pb(self, n):
        return View(self.buf, self.ap.partition_broadcast(n))

    @property
    def shape(self):
        return self.ap.shape


class Buf:
    __slots__ = ("t", "name", "w", "r", "dsem", "dcnt", "dram")

    def __init__(self, t, name, dram=False):
        self.t = t
        self.name = name
        self.w = None
        self.r = []
        self.dsem = None
        self.dcnt = 0
        self.dram = dram

    def __getitem__(self, k):
        return View(self, self.t[k])

    @property
    def v(self):
        return View(self, self.t if self.dram else self.t[:])


def _ap(x):
    return x.ap if isinstance(x, View) else x


def _bufs(xs):
    out = []
    for x in xs:
        if isinstance(x, View) and x.buf not in out:
            out.append(x.buf)
    return out


class Prog:
    ENG = ("pe", "act", "dve", "pool", "sp")

    def __init__(self):
        self.nc = bass.Bass("TRN2", target_bir_lowering=False)
        nc = self.nc
        self.es = contextlib.ExitStack()
        self.es.enter_context(nc.allow_non_contiguous_dma(reason="small strided parameter loads"))
        self.scopes = [self.es]
        self.scope_bufs = [[]]
        self.eng = {"pe": nc.tensor, "act": nc.scalar, "dve": nc.vector,
                    "pool": nc.gpsimd, "sp": nc.sync}
        self.sem = {e: self.es.enter_context(nc.semaphore("s_" + e)) for e in self.ENG}
        self.cnt = {e: 0 for e in self.ENG}
        self.seen = {e: {} for e in self.ENG}
        self.nb = 0
        self.out_waits = []
        self.sem_pool = []
        self.nsem = 0

    def dram_in(self, name, shape, dt=F32):
        return self.nc.dram_tensor(name, list(shape), dt, kind="ExternalInput").ap()

    def dram_out(self, name, shape, dt=F32):
        return self.nc.dram_tensor(name, list(shape), dt, kind="ExternalOutput").ap()

    def sb(self, shape, dt=F32, name=None):
        self.nb += 1
        name = "sb%d_%s" % (self.nb, name or "")
        t = self.scopes[-1].enter_context(self.nc.sbuf_tensor(name, list(shape), dt))
        b = Buf(t, name)
        self.scope_bufs[-1].append(b)
        return b

    def ps(self, shape, dt=F32, name=None):
        self.nb += 1
        name = "ps%d_%s" % (self.nb, name or "")
        t = self.scopes[-1].enter_context(self.nc.psum_tensor(name, list(shape), dt))
        b = Buf(t, name)
        self.scope_bufs[-1].append(b)
        return b

    def dram(self, shape, dt=F32, name=None, addr_space="Local"):
        self.nb += 1
        name = "dr%d_%s" % (self.nb, name or "")
        t = self.nc.dram_tensor(name, list(shape), dt, kind="Internal", addr_space=addr_space)
        return Buf(t.ap(), name, dram=True)

    @contextlib.contextmanager
    def scope(self):
        st = contextlib.ExitStack()
        self.scopes.append(st)
        self.scope_bufs.append([])
        try:
            yield
        finally:
            bufs = self.scope_bufs.pop()
            self.barrier(bufs)
            for b in bufs:
                if b.dsem is not None:
                    self.sem_pool.append((b.dsem, b.dcnt))
                    b.dsem = None
            self.scopes.pop()
            st.close()

    def barrier(self, bufs=()):
        for e in self.ENG:
            for f in self.ENG:
                if f != e and self.cnt[f] > 0:
                    self._wait(e, ("eng", f, self.cnt[f]))
            for b in bufs:
                if b.dsem is not None and b.dcnt > 0:
                    self._wait(e, ("dma", b, b.dcnt))

    def _getsem(self, b):
        if b.dsem is None:
            if self.sem_pool and not b.dram:
                b.dsem, b.dcnt = self.sem_pool.pop()
            else:
                self.nsem += 1
                b.dsem = self.es.enter_context(self.nc.semaphore("d%d" % self.nsem))
                b.dcnt = 0
        return b.dsem

    def _wait(self, e, dep):
        kind, key, val = dep
        if kind == "eng":
            if key == e and e == "pe":
                return
            sem = self.sem[key]
            skey = key
        else:
            sem = key.dsem
            if sem is None:
                return
            skey = "d:%d" % id(sem)
        if self.seen[e].get(skey, 0) >= val:
            return
        self.eng[e].wait_ge(sem, val)
        self.seen[e][skey] = val

    def _deps(self, e, reads, writes, dma=False):
        for b in reads:
            if b.w is not None:
                self._wait(e, b.w)
        for b in writes:
            if b.w is not None:
                if not (dma and b.w[0] == "dma"):
                    self._wait(e, b.w)
            for d in b.r:
                self._wait(e, d)

    def op(self, e, ins, reads=(), writes=()):
        reads = _bufs(reads)
        writes = _bufs(writes)
        self._deps(e, reads, writes)
        inst = ins(self.eng[e])
        self.cnt[e] += 1
        inst.then_inc(self.sem[e], 1)
        dep = ("eng", e, self.cnt[e])
        for b in reads:
            b.r = [d for d in b.r if not (d[0] == "eng" and d[1] == e)] + [dep]
        for b in writes:
            b.w = dep
            b.r = []
        return inst

    def dma(self, q, out, in_, **kw):
        reads = _bufs([in_])
        writes = _bufs([out])
        self._deps(q, reads, writes, dma=True)
        bufs = reads + writes
        own = [x for x in bufs if not x.dram]
        b = own[0] if own else bufs[0]
        self._getsem(b)
        inst = self.eng[q].dma_start(out=_ap(out), in_=_ap(in_), **kw)
        b.dcnt += 16
        inst.then_inc(b.dsem, 16)
        dep = ("dma", b, b.dcnt)
        for x in reads:
            x.r = [d for d in x.r if not (d[0] == "dma" and d[1] is b)] + [dep]
        for x in writes:
            x.w = dep
            x.r = []
        if not writes or any(x.dram for x in writes):
            self.out_waits.append(dep)
        return inst

    def allgather(self, src, dst, n=8):
        self._deps("pool", [src], [dst])
        self._getsem(dst)
        inst = self.nc.gpsimd.collective_compute(
            "AllGather", op=ALU.bypass, replica_groups=[list(range(n))],
            ins=[src.t.opt()], outs=[dst.t.opt()])
        dst.dcnt += 1
        inst.then_inc(dst.dsem, 1)
        dep = ("dma", dst, dst.dcnt)
        src.r = src.r + [dep]
        dst.w = dep
        dst.r = []
        self.out_waits.append(dep)
        return inst

    def finish(self):
        last = {}
        for kind, b, val in self.out_waits:
            if b.dsem is None:
                continue
            k = id(b)
            if k not in last or last[k][2] < val:
                last[k] = (kind, b, val)
        for dep in last.values():
            if dep[2] <= dep[1].dcnt:
                self._wait("sp", dep)
        for e in self.ENG:
            if e != "sp" and self.cnt[e] > 0:
                self._wait("sp", ("eng", e, self.cnt[e]))
        self.es.close()
        return self.nc

    def mm(self, out, lhsT, rhs, start=True, stop=True):
        return self.op("pe", lambda E: E.matmul(out.ap, lhsT.ap, rhs.ap, start=start, stop=stop),
                       reads=[lhsT, rhs], writes=[out])

    def tr(self, out, in_, ident):
        return self.op("pe", lambda E: E.transpose(out.ap, in_.ap, ident.ap), reads=[in_, ident], writes=[out])

    def tt(self, e, out, a, b, op):
        return self.op(e, lambda E: E.tensor_tensor(out=out.ap, in0=a.ap, in1=b.ap, op=op), reads=[a, b], writes=[out])

    def ts(self, e, out, a, s1, op0, s2=None, op1=None, accum=None):
        kw = {}
        if op1 is not None:
            kw["op1"] = op1
        if accum is not None:
            kw["accum_out"] = accum.ap
        return self.op(e, lambda E: E.tensor_scalar(out=out.ap, in0=a.ap, scalar1=_ap(s1), scalar2=_ap(s2), op0=op0, **kw),
                       reads=[a, s1, s2], writes=[out] + ([accum] if accum is not None else []))

    def stt(self, out, a, s, b, op0, op1):
        return self.op("dve", lambda E: E.scalar_tensor_tensor(out=out.ap, in0=a.ap, scalar=_ap(s), in1=b.ap, op0=op0, op1=op1),
                       reads=[a, s, b], writes=[out])

    def act(self, out, a, func, bias=None, scale=1.0, accum=None):
        kw = {}
        if bias is not None:
            kw["bias"] = _ap(bias)
        if accum is not None:
            kw["accum_out"] = accum.ap
        return self.op("act", lambda E: E.activation(out=out.ap, in_=a.ap, func=func, scale=_ap(scale), **kw),
                       reads=[a, bias, scale], writes=[out] + ([accum] if accum is not None else []))

    def cp(self, e, out, a):
        if e == "act":
            return self.op("act", lambda E: E.copy(out=out.ap, in_=a.ap), reads=[a], writes=[out])
        return self.op(e, lambda E: E.tensor_copy(out=out.ap, in_=a.ap), reads=[a], writes=[out])

    def memset(self, e, out, val):
        return self.op(e, lambda E: E.memset(out.ap, val), writes=[out])

    def recip(self, out, a):
        return self.op("dve", lambda E: E.reciprocal(out=out.ap, in_=a.ap), reads=[a], writes=[out])


import math

NCORE = 8
D = 2048
NT = 1088
TB = 4352
EPS = 1e-6
TG = [(0, 512), (512, 512), (1024, 64)]
GRP = ["rgx", "rgg", "q", "k", "qsw", "ksw", "hq", "hg", "zf", "zb", "v", "hv"]
GI = {n: i for i, n in enumerate(GRP)}


def tiles_of(n, step=128):
    return [(s, min(step, n - s)) for s in range(0, n, step)]


class MK:
    def __init__(self, layers=(0, 1, 2, 3), stages=None, dbg=()):
        self.P = Prog()
        self.layers = layers
        self.stages = stages
        self.dbg = dbg
        P = self.P
        NL = len(layers)
        self.lidx = {l: i for i, l in enumerate(layers)}
        shapes = {}

        def inp(name, shape):
            shapes[name] = shape
        inp("x", [NT, D]); inp("cvT", [128, 16, 3]); inp("wada", [NL, D, 1536]); inp("bada", [NL, 1536])
        inp("gmix", [NL, D]); inp("gffn", [NL, D]); inp("gfin", [D])
        inp("wc", [NL, D, 1536]); inp("wg", [NL, 256, 6144]); inp("bgT", [128, NL, 48])
        inp("rgcw", [128, NL, 4]); inp("rgcb", [128, NL]); inp("rgwa", [NL, 2, 128, 128]); inp("rgwx", [NL, 2, 128, 128])
        inp("rgba", [128, NL, 2]); inp("rgbx", [128, NL, 2]); inp("rglam", [128, NL, 2])
        inp("dalq", [NL, 128]); inp("dalk", [NL, 128]); inp("dasg", [128, NL])
        inp("hglbF", [128, 2, 4]); inp("hglbT", [2, 4, 128]); inp("hgon", [128, NL])
        inp("cosF", [128, 4096]); inp("sinF", [128, 4096])
        inp("wbr", [NL, 384, D]); inp("wout", [NL, 256, D]); inp("wq", [NL, 256, D]); inp("ut", [NL, 256, 16384])
        inp("vv", [NL, 2048, D]); inp("skT", [NL, 2, 128, 128])
        inp("ident", [128, 128]); inp("hgc", [12, 128, 128]); inp("ci", [128, 2])
        for k in list(shapes):
            shapes["inj_" + k] = None

        class Lazy(dict):
            def __missing__(d, name):
                d[name] = P.dram_in(name, shapes[name])
                return d[name]
        I = self.I = Lazy()
        self.shapes = shapes
        self.out = P.dram_out("out", [1024, D])
        self.dbg_out = {}

        self.pid = P.nc.gpsimd.partition_id()
        self.off384 = self.pid * 384
        S = self.S = {}
        S["xres"] = P.dram([NT, D], name="xres")
        S["mods_loc"] = P.dram([12, 1536], name="mods_loc")
        S["mods_all"] = P.dram([96, 1536], name="mods_all")
        S["hT_loc"] = P.dram([D, NT], BF16, name="hT_loc")
        S["hT_all"] = P.dram([8 * D, NT], BF16, name="hT_all")
        S["gatesT"] = P.dram([6144, NT], name="gatesT")
        for n in ("rgx", "rgg", "q", "k", "hq", "hg", "zfF", "zbF"):
            S[n] = P.dram([128, 2 * TB], BF16 if n in ("q", "k") else F32, name=n)
        for n in ("v", "zf", "zb", "hv"):
            S[n] = P.dram([2 * TB, 128], BF16 if n == "v" else F32, name=n)
        S["y_loc"] = P.dram([8 * 384, NT], name="y_loc")
        S["y_all"] = P.dram([8 * 3072, NT], name="y_all")
        S["y_in"] = P.dram([3072, NT], name="y_in")
        S["mT"] = P.dram([D, NT], name="mT")
        S["h2T"] = P.dram([D, NT], name="h2T")
        S["h2Tb"] = P.dram([D, NT], BF16, name="h2Tb")
        S["scores"] = P.dram([NT, 2048], name="scores")
        S["GT"] = P.dram([16384, NT], name="GT")
        S["WT"] = P.dram([16384, NT], BF16, name="WT")
        C = self.C = {}
        C["ident"] = P.sb([128, 128], name="ident")
        P.dma("sp", C["ident"].v, I["ident"])
        C["ones"] = P.sb([128, 128], name="ones")
        P.memset("dve", C["ones"].v, 1.0)
        C["onesb"] = P.sb([128, 128], BF16, name="onesb")
        P.memset("dve", C["onesb"].v, 1.0)

    def tap(self, name, buf):
        if name in self.dbg:
            o = self.P.dram_out("dbg_" + name, list(buf.t.shape))
            self.P.dma("sp", o, buf.v)

    def gather_weights(self, l):
        P, I = self.P, self.I
        W = {}
        for nm, rows, cols in (("wg", 256, 6144), ("wbr", 384, D), ("wout", 256, D), ("wq", 256, D),
                               ("ut", 256, 16384), ("vv", 2048, D)):
            bnc = P.dram([rows, cols], name="bn_%s%d" % (nm, l))
            full = P.dram([8 * rows, cols], name="g_%s%d" % (nm, l))
            npc = 4
            step = rows // npc
            for i in range(npc):
                P.dma("pool", bnc[i * step:(i + 1) * step, :], I[nm][self.lidx[l], i * step:(i + 1) * step, :])
            P.allgather(bnc, full)
            W[nm] = full
        return W

    def load_modvec(self, dst, l, kind, i):
        P = self.P
        ma = self.S["mods_all"]
        c0 = i * 2048
        pos = c0
        while pos < c0 + 2048:
            r = pos // 1536
            e = min((r + 1) * 1536, c0 + 2048)
            src = ma[r * 12 + self.lidx[l] * 3 + kind, pos - r * 1536:e - r * 1536].pb(128)
            P.dma("act", dst[:, pos - c0:e - c0], src)
            pos = e

    def emit_lmod(self):
        P, I, S = self.P, self.I, self.S
        with P.scope():
            sc = P.sb([128, 16, 3]); scs = P.sb([128, 16, 3])
            P.dma("sp", sc.v, I["cvT"])
            P.act(scs.v, sc.v, AF.Silu)
            wb = [P.sb([128, 16, 512], name="w%d" % i) for i in range(2)]
            bb = [P.sb([3, 512], name="bb%d" % i) for i in range(2)]
            ob = [P.sb([3, 512], name="ob%d" % i) for i in range(2)]
            pss = [P.ps([3, 512], name="ps%d" % i) for i in range(2)]
            it = 0
            for l in range(len(self.layers)):
                for n in range(3):
                    i = it % 2; it += 1
                    src = I["wada"][l].rearrange("(k p) c -> p k c", p=128)[:, :, n * 512:(n + 1) * 512]
                    for h in range(2):
                        P.dma("sp", wb[i][:, h * 8:(h + 1) * 8, :], src[:, h * 8:(h + 1) * 8, :])
                    P.dma("sp", bb[i].v, I["bada"][l, n * 512:(n + 1) * 512].partition_broadcast(3))
                    for k in range(16):
                        P.mm(pss[i].v, scs[:, k, :], wb[i][:, k, :], start=(k == 0), stop=(k == 15))
                    P.tt("dve", ob[i].v, pss[i].v, bb[i].v, ALU.add)
                    P.dma("sp", S["mods_loc"][l * 3:(l + 1) * 3, n * 512:(n + 1) * 512], ob[i].v)
        P.allgather(S["mods_loc"], S["mods_all"])
        self.tap("mods_all", S["mods_all"])

    def norm_rows(self, xt, rows, h, ss, A, B=None):
        P = self.P
        P.act(h, xt, AF.Square, accum=ss)
        P.ts("dve", ss, ss, 1.0 / D, ALU.mult, EPS, ALU.add)
        P.act(ss, ss, AF.Sqrt)
        P.recip(ss, ss)
        P.stt(h, xt, ss, A, ALU.mult, ALU.mult)
        if B is not None:
            P.tt("dve", h, h, B, ALU.add)

    def make_AB(self, l, gname, i_shift, i_scale, tmp):
        P, I = self.P, self.I
        P.dma("act", tmp.v, I[gname][self.lidx[l]].partition_broadcast(128))
        out = {}
        for kind in range(3):
            A = P.sb([128, D], name="A%d" % kind)
            B = P.sb([128, D], name="B%d" % kind)
            self.load_modvec(A.v, l, kind, i_scale)
            self.load_modvec(B.v, l, kind, i_shift)
            P.stt(A.v, A.v, 1.0, tmp.v, ALU.add, ALU.mult)
            out[kind] = (A, B)
        return out


def kind_of_tile(t0):
    return 0 if t0 < 512 else (1 if t0 < 1024 else 2)


class MK2(MK):
    def emit_t1a(self, l):
        P, I, S, C = self.P, self.I, self.S, self.C
        with P.scope():
            xt = [P.sb([128, D], name="xt%d" % i) for i in range(2)]
            h = P.sb([128, D], name="h")
            AB = self.make_AB(l, "gmix", 0, 1, xt[0])
            ss = [P.sb([128, 1], name="ss%d" % i) for i in range(2)]
            hTt = [P.sb([128, 16, 128], BF16, name="hTt%d" % i) for i in range(2)]
            pst = [P.ps([128, 512], name="pst%d" % i) for i in range(2)]
            ntr = 0
            for ti, (t0, rows) in enumerate(tiles_of(NT)):
                x_ = xt[ti % 2]; s_ = ss[ti % 2]; hT = hTt[ti % 2]
                P.dma("sp", x_[:rows, :], S["xres"][t0:t0 + rows, :])
                A, B = AB[kind_of_tile(t0)]
                self.norm_rows(x_[:rows, :], rows, h[:rows, :], s_[:rows, :], A[:rows, :], B[:rows, :])
                for q in range(4):
                    pt = pst[ntr % 2]; ntr += 1
                    for kk in range(4):
                        k = q * 4 + kk
                        P.tr(pt[:, kk * 128:kk * 128 + rows], h[:rows, k * 128:(k + 1) * 128], C["ident"][:rows, :rows])
                    P.cp("act" if q % 2 == 0 else "dve", hT[:, q * 4:(q + 1) * 4, :rows],
                         pt.v.r("p (a b) -> p a b", a=4)[:, :, :rows])
                P.dma("sp", S["hT_loc"].v.r("(k p) t -> p k t", p=128)[:, :, t0:t0 + rows], hT[:, :, :rows])
        P.allgather(S["hT_loc"], S["hT_all"])
        self.tap("hT_all", S["hT_all"])

    def emit_t1b(self, l, W):
        P, I, S, C = self.P, self.I, self.S, self.C
        with P.scope():
            hT = P.sb([128, 16, NT], BF16, name="hT")
            for q in range(4):
                P.dma("sp", hT[:, q * 4:(q + 1) * 4, :], S["hT_loc"].v.r("(k p) t -> p k t", p=128)[:, q * 4:(q + 1) * 4, :])
            wbb = [P.sb([128, 16, 512], BF16, name="wbb%d" % i) for i in range(2)]
            bg = P.sb([128, 48], name="bg")
            P.dma("act", bg.v, I["bgT"][:, self.lidx[l], :])
            wb = [P.sb([128, 16, 512], name="w%d" % i) for i in range(2)]
            st = [P.sb([128, 512], name="st%d" % i) for i in range(4)]
            pm = [P.ps([128, 512], name="pm%d" % i) for i in range(4)]
            wsrc = W["wg"].v.r("(k p) c -> p k c", p=128)

            def loadw(n):
                for hh in range(4):
                    P.dma("sp", wb[n % 2][:, hh * 4:(hh + 1) * 4, :], wsrc[:, hh * 4:(hh + 1) * 4, n * 512:(n + 1) * 512])
                P.cp("pool", wbb[n % 2][:, 0:8, :], wb[n % 2][:, 0:8, :])
                P.cp("dve", wbb[n % 2][:, 8:16, :], wb[n % 2][:, 8:16, :])
            loadw(0)
            it = 0
            for n in range(12):
                if n + 1 < 12:
                    loadw(n + 1)
                for c4 in range(4):
                    ch = n * 4 + c4
                    for (t0, tn) in TG:
                        p_ = pm[it % 4]; s_ = st[it % 4]; it += 1
                        for k in range(16):
                            P.mm(p_[:, :tn], wbb[n % 2][:, k, c4 * 128:(c4 + 1) * 128], hT[:, k, t0:t0 + tn],
                                 start=(k == 0), stop=(k == 15))
                        P.act(s_[:, :tn], p_[:, :tn], AF.Sigmoid, bias=bg[:, ch:ch + 1])
                        P.dma("sp", S["gatesT"][ch * 128:(ch + 1) * 128, t0:t0 + tn], s_[:, :tn])
        self.tap("gatesT", S["gatesT"])

    def emit_t1c(self, l):
        P, I, S, C = self.P, self.I, self.S, self.C
        with P.scope():
            wc = P.sb([128, 16, 1536], BF16, name="wc")
            wst = [P.sb([128, 16, 256], name="wst%d" % i) for i in range(2)]
            wsrc = I["wc"][self.lidx[l]].rearrange("(k p) c -> p k c", p=128)
            for q in range(6):
                for hh in range(2):
                    P.dma("sp", wst[q % 2][:, hh * 8:(hh + 1) * 8, :], wsrc[:, hh * 8:(hh + 1) * 8, q * 256:(q + 1) * 256])
                P.cp("act" if q % 2 == 0 else "pool", wc[:, :, q * 256:(q + 1) * 256], wst[q % 2].v)
            hTb = [P.sb([128, 16, 512], BF16, name="hTb%d" % i) for i in range(2)]
            stb = [P.sb([128, 512], BF16, name="stb%d" % i) for i in range(3)]
            svb = [P.sb([128, 128], BF16, name="svb%d" % i) for i in range(2)]
            cs = [P.sb([128, 512], name="cs%d" % i) for i in range(2)]
            sn = [P.sb([128, 512], name="sn%d" % i) for i in range(2)]
            st = [P.sb([128, 512], name="st%d" % i) for i in range(4)]
            tmp = [P.sb([128, 512], name="tmp%d" % i) for i in range(2)]
            pm = [P.ps([128, 512], name="pm%d" % i) for i in range(6)]
            it = 0
            ic = 0
            Fdst = {"rgx": "rgx", "rgg": "rgg", "hq": "hq", "hg": "hg", "zf": "zfF", "zb": "zbF"}
            for r in range(8):
                for ci, (t0, tn) in enumerate(TG):
                    hb = hTb[ic % 2]; cb = cs[ic % 2]; sb_ = sn[ic % 2]; ic += 1
                    src = S["hT_all"].v[r * D:(r + 1) * D, t0:t0 + tn].r("(k p) t -> p k t", p=128)
                    for q in range(4):
                        P.dma("sp", hb[:, q * 4:(q + 1) * 4, :tn], src[:, q * 4:(q + 1) * 4, :])
                    if ci < 2:
                        P.dma("act", cb.v, I["cosF"][:, r * 512:(r + 1) * 512])
                        P.dma("act", sb_.v, I["sinF"][:, r * 512:(r + 1) * 512])
                        segs = [(0, tn, ci * TB + 256 + r * 512)]
                    else:
                        segs = [(0, 32, r * 32), (32, 32, TB + r * 32)]

                    def fmm(gname):
                        nonlocal it
                        p_ = pm[it % 6]; it += 1
                        g0 = GI[gname] * 128
                        for k in range(16):
                            P.mm(p_[:, :tn], wc[:, k, g0:g0 + 128], hb[:, k, :tn], start=(k == 0), stop=(k == 15))
                        return p_

                    for gname in ("rgx", "rgg", "hq", "hg", "zf", "zb"):
                        p_ = fmm(gname)
                        s_ = st[it % 4]
                        P.cp("act" if it % 2 == 0 else "dve", s_[:, :tn], p_[:, :tn])
                        for (c0, n, d0) in segs:
                            P.dma("sp", S[Fdst[gname]][:, d0:d0 + n], s_[:, c0:c0 + n])
                    for gname, gsw in (("q", "qsw"), ("k", "ksw")):
                        p1 = fmm(gname)
                        s_ = stb[it % 3]
                        if ci < 2:
                            p2 = fmm(gsw)
                            P.tt("dve", tmp[0][:, :tn], p1[:, :tn], cb[:, :tn], ALU.mult)
                            P.tt("dve", tmp[1][:, :tn], p2[:, :tn], sb_[:, :tn], ALU.mult)
                            P.tt("pool", s_[:, :tn], tmp[0][:, :tn], tmp[1][:, :tn], ALU.add)
                        else:
                            P.cp("act", s_[:, :tn], p1[:, :tn])
                        for (c0, n, d0) in segs:
                            P.dma("sp", S[gname][:, d0:d0 + n], s_[:, c0:c0 + n])
                    for (a0, rows) in tiles_of(tn):
                        p_ = pm[it % 6]; it += 1
                        s_ = st[it % 4]
                        for k in range(16):
                            P.mm(p_[:rows, :], hb[:, k, a0:a0 + rows], wc[:, k, 8 * 128:12 * 128], start=(k == 0), stop=(k == 15))
                        P.cp("act" if it % 2 == 0 else "dve", s_[:rows, :], p_[:rows, :])
                        sv = svb[it % 2]
                        P.cp("pool", sv[:rows, :], s_[:rows, 256:384])
                        for (c0, n, d0) in segs:
                            lo = max(c0, a0); hi = min(c0 + n, a0 + rows)
                            if lo >= hi:
                                continue
                            for gi, dn in enumerate(("zf", "zb", "v", "hv")):
                                if dn == "v":
                                    P.dma("sp", S[dn][d0 + lo - c0:d0 + hi - c0, :], sv[lo - a0:hi - a0, :])
                                else:
                                    P.dma("sp", S[dn][d0 + lo - c0:d0 + hi - c0, :], s_[lo - a0:hi - a0, gi * 128:(gi + 1) * 128])
        for n in ("rgx", "q", "k", "v", "zf", "zfF", "hq"):
            self.tap(n, S[n])


def col_groups(n, step=512):
    return [(s, min(step, n - s)) for s in range(0, n, step)]


class MK3(MK2):
    def emit_rg(self, l):
        P, I, S, C = self.P, self.I, self.S, self.C
        with P.scope():
            cw = P.sb([128, 4], name="cw"); cb = P.sb([128, 1], name="cb")
            ba = P.sb([128, 2], name="ba"); bx = P.sb([128, 2], name="bx"); lam = P.sb([128, 2], name="lam")
            nsp8 = P.sb([128, 2], name="nsp8")
            wa = P.sb([128, 2, 128], name="wa"); wx = P.sb([128, 2, 128], name="wx")
            P.dma("act", cw.v, I["rgcw"][:, self.lidx[l], :]); P.dma("act", cb.v, I["rgcb"][:, self.lidx[l]:self.lidx[l] + 1])
            P.dma("act", ba.v, I["rgba"][:, self.lidx[l], :]); P.dma("act", bx.v, I["rgbx"][:, self.lidx[l], :])
            P.dma("act", lam.v, I["rglam"][:, self.lidx[l], :])
            P.dma("act", wa.v, I["rgwa"][self.lidx[l]].rearrange("d p e -> p d e"))
            P.dma("act", wx.v, I["rgwx"][self.lidx[l]].rearrange("d p e -> p d e"))
            P.act(nsp8.v, lam.v, AF.Exp, scale=-1.0)
            P.ts("dve", nsp8.v, nsp8.v, 1.0, ALU.add)
            P.act(nsp8.v, nsp8.v, AF.Ln)
            P.ts("dve", nsp8.v, nsp8.v, -8.0, ALU.mult)
            x = P.sb([128, TB], name="x"); g = P.sb([128, TB], name="g"); u = P.sb([128, TB], name="u")
            ra = P.sb([128, TB], name="ra"); ix = P.sb([128, TB], name="ix"); m = P.sb([128, TB], name="m")
            hf = P.sb([128, TB], name="hf"); hb = P.sb([128, TB], name="hb")
            pp = [P.ps([128, 512], name="pp%d" % i) for i in range(4)]
            it = 0
            for b in range(2):
                for q in range(2):
                    h0 = q * (TB // 2)
                    P.dma("sp", x[:, h0:h0 + TB // 2], S["rgx"][:, b * TB + h0:b * TB + h0 + TB // 2])
                    P.dma("sp", g[:, h0:h0 + TB // 2], S["rgg"][:, b * TB + h0:b * TB + h0 + TB // 2])
                for (a, e) in ((0, 256), (256, TB)):
                    P.ts("dve", u[:, a:e], x[:, a:e], cw[:, 2:3], ALU.mult, cb[:, 0:1], ALU.add)
                    P.stt(u[:, a + 2:e], x[:, a:e - 2], cw[:, 0:1], u[:, a + 2:e], ALU.mult, ALU.add)
                    P.stt(u[:, a + 1:e], x[:, a:e - 1], cw[:, 1:2], u[:, a + 1:e], ALU.mult, ALU.add)
                    P.stt(u[:, a:e - 1], x[:, a + 1:e], cw[:, 3:4], u[:, a:e - 1], ALU.mult, ALU.add)
                for d in range(2):
                    for (c0, n) in col_groups(TB):
                        p1 = pp[it % 4]; it += 1
                        P.mm(p1[:, :n], wa[:, d, :], u[:, c0:c0 + n])
                        P.act(ra[:, c0:c0 + n], p1[:, :n], AF.Sigmoid, bias=ba[:, d:d + 1])
                        p2 = pp[it % 4]; it += 1
                        P.mm(p2[:, :n], wx[:, d, :], u[:, c0:c0 + n])
                        P.act(ix[:, c0:c0 + n], p2[:, :n], AF.Sigmoid, bias=bx[:, d:d + 1])
                    P.act(ra.v, ra.v, AF.Exp, scale=nsp8[:, d:d + 1])
                    P.tt("dve", m.v, ra.v, ra.v, ALU.mult)
                    P.ts("dve", m.v, m.v, -1.0, ALU.mult, 1.0, ALU.add)
                    P.act(m.v, m.v, AF.Sqrt)
                    P.tt("dve", m.v, m.v, ix.v, ALU.mult)
                    P.tt("dve", m.v, m.v, u.v, ALU.mult)
                    if d == 0:
                        init = 0.0
                        for (c0, n) in col_groups(TB, 2048):
                            iv = init
                            P.op("dve", lambda E: E.tensor_tensor_scan(out=hf[:, c0:c0 + n].ap, data0=ra[:, c0:c0 + n].ap,
                                                                        data1=m[:, c0:c0 + n].ap, initial=_ap(iv),
                                                                        op0=ALU.mult, op1=ALU.add),
                                 reads=[ra.v, m.v, hf.v], writes=[hf.v])
                            init = hf[:, c0 + n - 1:c0 + n]
                    else:
                        init = 0.0
                        for (a, e) in ((0, 256), (2304, TB), (256, 2304)):
                            iv = init
                            P.op("dve", lambda E: E.tensor_tensor_scan(out=hb[:, a:e][:, ::-1].ap, data0=ra[:, a:e][:, ::-1].ap,
                                                                        data1=m[:, a:e][:, ::-1].ap, initial=_ap(iv),
                                                                        op0=ALU.mult, op1=ALU.add),
                                 reads=[ra.v, m.v, hb.v], writes=[hb.v])
                            init = hb[:, a:a + 1]
                P.tt("dve", hf.v, hf.v, hb.v, ALU.add)
                P.act(g.v, g.v, AF.Gelu_apprx_tanh)
                P.tt("dve", hf.v, hf.v, g.v, ALU.mult)
                self.store_y(0, b, hf.v)

    def emit_da(self, l):
        P, I, S, C = self.P, self.I, self.S, self.C
        li = 0.8 - 0.6 * math.exp(-0.3 * l)
        with P.scope():
            lq = P.sb([128, 128], name="lq"); lk = P.sb([128, 128], name="lk")
            P.dma("act", lq.v, I["dalq"][self.lidx[l]].partition_broadcast(128))
            P.dma("act", lk.v, I["dalk"][self.lidx[l]].partition_broadcast(128))
            P.tt("dve", lq.v, lq.v, lk.v, ALU.mult)
            e2 = P.sb([128, 2], name="e2")
            P.op("dve", lambda E: E.tensor_reduce(out=e2.v.ap, in_=lq.v.r("p (j d) -> p j d", j=2).ap, op=ALU.add, axis=AX.X),
                 reads=[lq.v], writes=[e2.v])
            P.act(e2.v, e2.v, AF.Exp)
            nlam = P.sb([128, 1], name="nlam")
            P.tt("dve", nlam.v, e2[:, 1:2], e2[:, 0:1], ALU.subtract)
            P.ts("dve", nlam.v, nlam.v, -li, ALU.add)
            gs = P.sb([128, 1], name="gs")
            P.dma("act", gs.v, I["dasg"][:, self.lidx[l]:self.lidx[l] + 1])
            P.ts("dve", gs.v, gs.v, 1.0 - li, ALU.mult)
            kT = P.sb([128, TB], BF16, name="kT"); vb = P.sb([128, 34, 128], BF16, name="vb")
            qT = [P.sb([128, 512], BF16, name="qT%d" % i) for i in range(2)]
            pT = [P.sb([128, 512], BF16, name="pT%d" % i) for i in range(3)]
            o0 = P.sb([128, 512], name="o0"); o1 = P.sb([128, 512], name="o1"); rd = P.sb([128, 512], name="rd")
            yo = [P.sb([128, 512], name="yo%d" % i) for i in range(2)]
            psS = [P.ps([128, 512], name="psS%d" % i) for i in range(2)]
            psO = [P.ps([128, 512], name="psO%d" % i) for i in range(2)]
            psD = [P.ps([128, 512], name="psD%d" % i) for i in range(2)]
            psM = P.ps([128, 512], name="psM")
            iq = 0; ip = 0; iss = 0
            for b in range(2):
                for q in range(2):
                    h0 = q * (TB // 2)
                    P.dma("sp", kT[:, h0:h0 + TB // 2], S["k"][:, b * TB + h0:b * TB + h0 + TB // 2])
                P.dma("sp", vb.v, S["v"][b * TB:(b + 1) * TB, :].r("(t p) d -> p t d", p=128))
                qgs = [(0, 256, 2)] + [(256 + i * 512, 512, 34) for i in range(8)]
                for (q0, qn, nkt) in qgs:
                    qt = qT[iq % 2]; iq += 1
                    P.dma("sp", qt[:, :qn], S["q"][:, b * TB + q0:b * TB + q0 + qn])
                    for kt in range(nkt):
                        for j in range(2):
                            ps = psS[iss % 2]; iss += 1
                            P.mm(ps[:, :qn], kT[j * 64:(j + 1) * 64, kt * 128:(kt + 1) * 128], qt[j * 64:(j + 1) * 64, :qn])
                            pt = pT[ip % 3]; ip += 1
                            P.act(pt[:, :qn], ps[:, :qn], AF.Exp, scale=0.125)
                            P.mm(psO[j][:, :qn], vb[:, kt, :], pt[:, :qn], start=(kt == 0), stop=(kt == nkt - 1))
                            P.mm(psD[j][:, :qn], C["onesb"].v, pt[:, :qn], start=(kt == 0), stop=(kt == nkt - 1))
                    P.recip(rd[:, :qn], psD[0][:, :qn])
                    P.tt("dve", o0[:, :qn], psO[0][:, :qn], rd[:, :qn], ALU.mult)
                    P.recip(rd[:, :qn], psD[1][:, :qn])
                    P.tt("dve", o1[:, :qn], psO[1][:, :qn], rd[:, :qn], ALU.mult)
                    P.stt(o0[:, :qn], o1[:, :qn], nlam[:, 0:1], o0[:, :qn], ALU.mult, ALU.add)
                    P.tt("pool", o1[:, :qn], o0[:, :qn], o0[:, :qn], ALU.mult)
                    P.mm(psM[:, :qn], C["ones"].v, o1[:, :qn])
                    P.ts("dve", rd[:, :qn], psM[:, :qn], 1.0 / 128, ALU.mult, EPS, ALU.add)
                    P.act(rd[:, :qn], rd[:, :qn], AF.Sqrt)
                    P.recip(rd[:, :qn], rd[:, :qn])
                    y_ = yo[iq % 2]
                    P.stt(y_[:, :qn], o0[:, :qn], gs[:, 0:1], rd[:, :qn], ALU.mult, ALU.mult)
                    self.store_y(1, b, y_[:, :qn], q0, qn)

    def emit_hg(self, l):
        P, I, S, C = self.P, self.I, self.S, self.C
        with P.scope():
            hc = P.sb([128, 12, 128], name="hgc")
            P.dma("act", hc.v, I["hgc"].rearrange("m p t -> p m t"))
            cib = P.sb([128, 2], name="ci")
            P.dma("act", cib.v, I["ci"])
            gon = P.sb([128, 1], name="gon")
            P.dma("act", gon.v, I["hgon"][:, self.lidx[l]:self.lidx[l] + 1])
            lbF = P.sb([128, 2], name="lbF"); omlF = P.sb([128, 2], name="omlF")
            lbT = P.sb([128, 2, 128], name="lbT"); omlT = P.sb([128, 2, 128], name="omlT")
            if l == 0:
                P.memset("dve", lbF.v, 0.0); P.memset("dve", lbT.v, 0.0)
            else:
                eF = P.sb([128, 2, 4], name="eF"); tF = P.sb([128, 2], name="tF")
                P.dma("act", eF.v, I["hglbF"])
                P.act(eF.v, eF.v, AF.Exp)
                P.tt("dve", tF.v, eF[:, :, 0], eF[:, :, 1], ALU.add)
                P.tt("dve", tF.v, tF.v, eF[:, :, 2], ALU.add)
                P.tt("dve", tF.v, tF.v, eF[:, :, 3], ALU.add)
                P.recip(tF.v, tF.v)
                P.cp("dve", lbF.v, eF[:, :, 1])
                for l2 in range(2, l + 1):
                    P.tt("dve", lbF.v, lbF.v, eF[:, :, l2], ALU.add)
                P.tt("dve", lbF.v, lbF.v, tF.v, ALU.mult)
                eT = P.sb([128, 2, 4, 128], name="eT"); tT = P.sb([128, 2, 128], name="tT")
                P.dma("act", eT.v.r("p a b c -> p (a b c)"), I["hglbT"].rearrange("a b c -> (a b c)").partition_broadcast(128))
                P.act(eT.v, eT.v, AF.Exp)
                P.tt("dve", tT.v, eT[:, :, 0, :], eT[:, :, 1, :], ALU.add)
                P.tt("dve", tT.v, tT.v, eT[:, :, 2, :], ALU.add)
                P.tt("dve", tT.v, tT.v, eT[:, :, 3, :], ALU.add)
                P.recip(tT.v, tT.v)
                P.cp("dve", lbT.v, eT[:, :, 1, :])
                for l2 in range(2, l + 1):
                    P.tt("dve", lbT.v, lbT.v, eT[:, :, l2, :], ALU.add)
                P.tt("dve", lbT.v, lbT.v, tT.v, ALU.mult)
            P.ts("dve", omlF.v, lbF.v, -1.0, ALU.mult, 1.0, ALU.add)
            P.ts("dve", omlT.v, lbT.v, -1.0, ALU.mult, 1.0, ALU.add)

            fT_ = P.sb([128, 34, 128], name="f"); logf = P.sb([128, 34, 128], name="logf"); kk = P.sb([128, 34, 128], name="kk")
            kkT = P.sb([128, TB], name="kkT"); qT = P.sb([128, TB], name="qT"); vb = P.sb([128, 34, 128], name="vb")
            oacc = P.sb([128, TB], name="oacc")
            Sst = [P.sb([128, 128], name="S%d" % i) for i in range(2)]
            tmp = {n: [P.sb([128, 128], name="%s%d" % (n, i)) for i in range(2)]
                   for n in ("emc", "khat", "ecum", "qhat", "e1", "e2", "qtil", "ktil", "atm", "cl", "qtc", "ktc", "atc")}
            etot = [P.sb([128, 2], name="etot%d" % i) for i in range(2)]
            pA = [P.ps([128, 512], name="pA%d" % i) for i in range(2)]
            pB = [P.ps([128, 512], name="pB%d" % i) for i in range(2)]
            pC = [P.ps([128, 512], name="pC%d" % i) for i in range(2)]
            pU = [P.ps([128, 512], name="pU%d" % i) for i in range(2)]
            for b in range(2):
                for q in range(2):
                    h0 = q * (TB // 2)
                    P.dma("sp", qT[:, h0:h0 + TB // 2], S["hq"][:, b * TB + h0:b * TB + h0 + TB // 2])
                P.dma("sp", vb.v, S["hv"][b * TB:(b + 1) * TB, :].r("(t p) d -> p t d", p=128))
                for d in range(2):
                    zn, znF = ("zf", "zfF") if d == 0 else ("zb", "zbF")
                    LT = hc[:, 6 * d + 0, :]; BmL = hc[:, 6 * d + 1, :]; Dw = hc[:, 6 * d + 2, :]
                    Dc = hc[:, 6 * d + 3, :]; Mw = hc[:, 6 * d + 4, :]; Mc = hc[:, 6 * d + 5, :]
                    P.dma("sp", fT_.v, S[zn][b * TB:(b + 1) * TB, :].r("(t p) d -> p t d", p=128))
                    for q in range(2):
                        h0 = q * (TB // 2)
                        P.dma("sp", kkT[:, h0:h0 + TB // 2], S[znF][:, b * TB + h0:b * TB + h0 + TB // 2])
                    P.act(fT_.v, fT_.v, AF.Sigmoid)
                    P.tt("dve", fT_.v, fT_.v, omlT[:, d, :].unsq(1).bc([128, 34, 128]), ALU.mult)
                    P.tt("dve", fT_.v, fT_.v, lbT[:, d, :].unsq(1).bc([128, 34, 128]), ALU.add)
                    P.ts("dve", fT_.v, fT_.v, 1e-20, ALU.max)
                    P.act(logf.v, fT_.v, AF.Ln)
                    P.ts("pool", kk.v, fT_.v, -1.0, ALU.mult, 1.0, ALU.add)
                    P.act(kkT.v, kkT.v, AF.Sigmoid)
                    P.ts("dve", kkT.v, kkT.v, omlF[:, d:d + 1], ALU.mult, lbF[:, d:d + 1], ALU.add)
                    P.ts("dve", kkT.v, kkT.v, -1.0, ALU.mult, 1.0, ALU.add)
                    si = 0
                    P.memset("dve", Sst[0].v, 0.0)
                    if d == 0:
                        order = list(range(34)); corder = (0, 1)
                    else:
                        order = [1, 0] + list(range(33, 1, -1)); corder = (1, 0)
                    for n_, ti in enumerate(order):
                        i2 = n_ % 2
                        c0 = ti * 128
                        lf = logf[:, ti, :]
                        T_ = {k: v[i2] for k, v in tmp.items()}
                        P.mm(pA[i2][:, 0:128], BmL, lf)
                        P.act(T_["emc"].v, pA[i2][:, 0:128], AF.Exp)
                        P.tt("pool", T_["khat"].v, kk[:, ti, :], T_["emc"].v, ALU.mult)
                        P.mm(pB[i2][:, 0:128], lf, LT)
                        P.act(T_["ecum"].v, pB[i2][:, 0:128], AF.Exp)
                        P.tt("dve", T_["qhat"].v, qT[:, c0:c0 + 128], T_["ecum"].v, ALU.mult)
                        P.mm(pC[i2][:, 0:128], lf, Dw)
                        P.ts("dve", T_["cl"].v, pC[i2][:, 0:128], -40.0, ALU.max, 40.0, ALU.min)
                        P.act(T_["e1"].v, T_["cl"].v, AF.Exp)
                        P.act(T_["e2"].v, T_["cl"].v, AF.Exp, scale=-1.0)
                        P.tt("dve", T_["qtil"].v, qT[:, c0:c0 + 128], T_["e1"].v, ALU.mult)
                        P.tt("pool", T_["ktil"].v, kkT[:, c0:c0 + 128], T_["e2"].v, ALU.mult)
                        P.mm(pA[i2][:, 0:128], T_["ktil"].v, T_["qtil"].v)
                        P.tt("dve", T_["atm"].v, pA[i2][:, 0:128], Mw, ALU.mult)
                        P.mm(pC[i2][:, 128:256], lf, Dc)
                        P.ts("dve", T_["cl"].v, pC[i2][:, 128:256], 0.0, ALU.min)
                        P.act(T_["e1"].v, T_["cl"].v, AF.Exp)
                        P.ts("dve", T_["cl"].v, pC[i2][:, 128:256], 0.0, ALU.max)
                        P.act(T_["e2"].v, T_["cl"].v, AF.Exp, scale=-1.0)
                        P.tt("dve", T_["qtc"].v, qT[:, c0:c0 + 128], T_["e1"].v, ALU.mult)
                        P.tt("pool", T_["ktc"].v, kkT[:, c0:c0 + 128], T_["e2"].v, ALU.mult)
                        P.mm(pA[i2][:, 128:256], T_["ktc"].v, T_["qtc"].v)
                        P.tt("dve", T_["atc"].v, pA[i2][:, 128:256], Mc, ALU.mult)
                        P.tt("pool", T_["atm"].v, T_["atm"].v, T_["atc"].v, ALU.add)
                        P.mm(pB[i2][:, 0:2], lf, cib.v)
                        P.act(etot[i2].v, pB[i2][:, 0:2], AF.Exp)
                        P.mm(pU[0][:, 0:128], T_["khat"][0:64, :], vb[0:64, ti, :])
                        P.mm(pU[1][:, 0:128], T_["khat"][64:128, :], vb[64:128, ti, :])
                        po = pC[i2]
                        P.mm(po[:, 0:128], vb[:, ti, :], T_["atm"].v, start=True, stop=False)
                        for n2, cc in enumerate(corder):
                            Scur = Sst[si % 2]; Snew = Sst[(si + 1) % 2]; si += 1
                            P.mm(po[:, cc * 64:(cc + 1) * 64], Scur.v, T_["qhat"][:, cc * 64:(cc + 1) * 64],
                                 start=False, stop=(n2 == 1))
                            P.stt(Snew.v, Scur.v, etot[i2][:, cc:cc + 1], pU[cc][:, 0:128], ALU.mult, ALU.add)
                        if d == 0:
                            P.cp("act", oacc[:, c0:c0 + 128], po[:, 0:128])
                        else:
                            P.tt("dve", oacc[:, c0:c0 + 128], oacc[:, c0:c0 + 128], po[:, 0:128], ALU.add)
                gT = kkT
                for q in range(2):
                    h0 = q * (TB // 2)
                    P.dma("sp", gT[:, h0:h0 + TB // 2], S["hg"][:, b * TB + h0:b * TB + h0 + TB // 2])
                P.act(gT.v, gT.v, AF.Silu)
                sq = logf.v.r("p a b -> p (a b)")
                rs = kk.v.r("p a b -> p (a b)")
                P.tt("pool", sq, oacc.v, oacc.v, ALU.mult)
                for gi, (c0, n) in enumerate(col_groups(TB)):
                    pm = pA[gi % 2]
                    P.mm(pm[:, :n], C["ones"].v, sq[:, c0:c0 + n])
                    P.ts("dve", rs[:, c0:c0 + n], pm[:, :n], 1.0 / 128, ALU.mult, EPS, ALU.add)
                P.act(rs, rs, AF.Sqrt)
                P.recip(rs, rs)
                P.stt(oacc.v, oacc.v, gon[:, 0:1], rs, ALU.mult, ALU.mult)
                P.tt("dve", oacc.v, oacc.v, gT.v, ALU.mult)
                self.store_y(2, b, oacc.v)

    def store_y(self, k, b, src, q0=0, qn=TB):
        P, S = self.P, self.S
        for j in range(8):
            for (ts_, n, loc) in ((256 + j * 512, 512, b * 512), (j * 32, 32, 1024 + b * 32)):
                lo = max(ts_, q0); hi = min(ts_ + n, q0 + qn)
                if lo >= hi:
                    continue
                P.dma("sp", S["y_loc"][j * 384 + k * 128:j * 384 + (k + 1) * 128, loc + lo - ts_:loc + hi - ts_],
                      src[:, lo - q0:hi - q0])

    def emit_yx(self):
        self.tap("y_loc", self.S["y_loc"])
        self.P.allgather(self.S["y_loc"], self.S["y_all"])
        P, S = self.P, self.S
        with P.scope():
            bb = [P.sb([128, 3, NT], name="yb%d" % i) for i in range(2)]
            for r in range(8):
                sl = S["y_all"].t[r * 3072:(r + 1) * 3072, :][bass.ds(self.off384, 384), :].rearrange("(k p) t -> p k t", p=128)
                b_ = bb[r % 2]
                P.dma("pool", b_.v, View(S["y_all"], sl))
                P.dma("sp", S["y_in"][r * 384:(r + 1) * 384, :].r("(k p) t -> p k t", p=128), b_.v)
        self.tap("y_in", S["y_in"])


class MK4(MK3):
    def emit_t2_merge(self, l, W):
        P, I, S, C = self.P, self.I, self.S, self.C
        with P.scope():
            yT = P.sb([128, 8, 3, 512], name="yT")
            wb = [P.sb([128, 24, 512], name="wbr%d" % i) for i in range(2)]
            gt = [P.sb([128, 512], name="gt%d" % i) for i in range(3)]
            acc = [P.sb([128, 512], name="acc%d" % i) for i in range(2)]
            tmp = [P.sb([128, 512], name="tmp%d" % i) for i in range(2)]
            pk = [P.ps([128, 512], name="pk%d" % i) for i in range(4)]
            wsrc = W["wbr"].v.r("(kr p) c -> p kr c", p=128)
            ig = 0; ia = 0; ip = 0; iw = 0
            for (t0, tn) in TG:
                for r in range(8):
                    P.dma("sp", yT[:, r, :, :tn], S["y_in"][r * 384:(r + 1) * 384, t0:t0 + tn].r("(k p) t -> p k t", p=128))
                for og in range(4):
                    w_ = wb[iw % 2]; iw += 1
                    for q in range(3):
                        P.dma("sp", w_[:, q * 8:(q + 1) * 8, :], wsrc[:, q * 8:(q + 1) * 8, og * 512:(og + 1) * 512])
                    for c4 in range(4):
                        oc = og * 4 + c4
                        a_ = acc[ia % 2]; ia += 1
                        for k in range(3):
                            p_ = pk[ip % 4]; ip += 1
                            for r in range(8):
                                P.mm(p_[:, :tn], w_[:, k * 8 + r, c4 * 128:(c4 + 1) * 128], yT[:, r, k, :tn],
                                     start=(r == 0), stop=(r == 7))
                            g_ = gt[ig % 3]; ig += 1
                            ch = k * 16 + oc
                            P.dma("act", g_[:, :tn], S["gatesT"][ch * 128:(ch + 1) * 128, t0:t0 + tn])
                            if k == 0:
                                P.tt("dve", a_[:, :tn], p_[:, :tn], g_[:, :tn], ALU.mult)
                            else:
                                t_ = tmp[k % 2]
                                P.tt("dve", t_[:, :tn], p_[:, :tn], g_[:, :tn], ALU.mult)
                                P.tt("pool", a_[:, :tn], a_[:, :tn], t_[:, :tn], ALU.add)
                        P.dma("sp", S["mT"][oc * 128:(oc + 1) * 128, t0:t0 + tn], a_[:, :tn])
        self.tap("mT", S["mT"])

    def emit_t2_wout(self, l, W):
        P, I, S, C = self.P, self.I, self.S, self.C
        with P.scope():
            mT = P.sb([128, 16, NT], name="mT")
            for q in range(4):
                P.dma("sp", mT[:, q * 4:(q + 1) * 4, :], S["mT"].v.r("(k p) t -> p k t", p=128)[:, q * 4:(q + 1) * 4, :])
            M2 = P.sb([128, D], name="M2"); A = P.sb([128, D], name="A"); B = P.sb([128, D], name="B")
            G = P.sb([128, D], name="G")
            P.dma("act", G.v, I["gffn"][self.lidx[l]].partition_broadcast(128))
            wo = [P.sb([128, 16, 512], name="wo%d" % i) for i in range(2)]
            xt = P.sb([128, D], name="xt"); h = P.sb([128, D], name="h"); ss = P.sb([128, 1], name="ss")
            tmp = [P.sb([128, 512], name="tmp%d" % i) for i in range(2)]
            hTt = [P.sb([128, 16, 128], name="hTt%d" % i) for i in range(2)]
            hTb2 = [P.sb([128, 16, 128], BF16, name="hTb2%d" % i) for i in range(1)]
            pm = [P.ps([128, 512], name="pm%d" % i) for i in range(4)]
            pst = [P.ps([128, 512], name="pst%d" % i) for i in range(2)]
            wsrc = W["wout"].v.r("(k p) c -> p k c", p=128)
            iw = 0; ip = 0; ntr = 0
            curk = -1
            for ti, (t0, rows) in enumerate(tiles_of(NT)):
                kind = kind_of_tile(t0)
                if kind != curk:
                    curk = kind
                    self.load_modvec(M2.v, l, kind, 2)
                    self.load_modvec(A.v, l, kind, 4)
                    self.load_modvec(B.v, l, kind, 3)
                    P.stt(A.v, A.v, 1.0, G.v, ALU.add, ALU.mult)
                P.dma("sp", xt[:rows, :], S["xres"][t0:t0 + rows, :])
                for n in range(4):
                    w_ = wo[iw % 2]; iw += 1
                    for q in range(4):
                        P.dma("sp", w_[:, q * 4:(q + 1) * 4, :], wsrc[:, q * 4:(q + 1) * 4, n * 512:(n + 1) * 512])
                    p_ = pm[ip % 4]; ip += 1
                    for k in range(16):
                        P.mm(p_[:rows, :], mT[:, k, t0:t0 + rows], w_[:, k, :], start=(k == 0), stop=(k == 15))
                    t_ = tmp[n % 2]
                    P.tt("dve", t_[:rows, :], p_[:rows, :], M2[:rows, n * 512:(n + 1) * 512], ALU.mult)
                    P.tt("pool", xt[:rows, n * 512:(n + 1) * 512], xt[:rows, n * 512:(n + 1) * 512], t_[:rows, :], ALU.add)
                P.dma("sp", S["xres"][t0:t0 + rows, :], xt[:rows, :])
                self.norm_rows(xt[:rows, :], rows, h[:rows, :], ss[:rows, :], A[:rows, :], B[:rows, :])
                hT = hTt[ti % 2]
                for q in range(4):
                    pt = pst[ntr % 2]; ntr += 1
                    for kk in range(4):
                        k = q * 4 + kk
                        P.tr(pt[:, kk * 128:kk * 128 + rows], h[:rows, k * 128:(k + 1) * 128], C["ident"][:rows, :rows])
                    P.cp("act" if q % 2 == 0 else "dve", hT[:, q * 4:(q + 1) * 4, :rows],
                         pt.v.r("p (a b) -> p a b", a=4)[:, :, :rows])
                P.dma("sp", S["h2T"].v.r("(k p) t -> p k t", p=128)[:, :, t0:t0 + rows], hT[:, :, :rows])
                hb_ = hTb2[0]
                P.cp("pool", hb_[:, :, :rows], hT[:, :, :rows])
                P.dma("sp", S["h2Tb"].v.r("(k p) t -> p k t", p=128)[:, :, t0:t0 + rows], hb_[:, :, :rows])
        self.tap("xres_mid", S["xres"]); self.tap("h2T", S["h2T"])

    def emit_t2_scores(self, l, W):
        P, I, S, C = self.P, self.I, self.S, self.C
        with P.scope():
            h2T = P.sb([128, 16, NT], name="h2T")
            for q in range(4):
                P.dma("sp", h2T[:, q * 4:(q + 1) * 4, :], S["h2T"].v.r("(k p) t -> p k t", p=128)[:, q * 4:(q + 1) * 4, :])
            skT = P.sb([128, 2, 128], name="skT")
            P.dma("act", skT.v, I["skT"][self.lidx[l]].rearrange("j d n -> d j n"))
            wq = [P.sb([128, 16, 512], name="wq%d" % i) for i in range(2)]
            qs = P.sb([128, D], name="qs"); qT = P.sb([128, 16, 128], name="qT"); sc = P.sb([128, D], name="sc")
            pm = [P.ps([128, 512], name="pm%d" % i) for i in range(4)]
            pst = [P.ps([128, 512], name="pst%d" % i) for i in range(2)]
            wsrc = W["wq"].v.r("(k p) c -> p k c", p=128)
            iw = 0; ip = 0; ntr = 0
            for ti, (t0, rows) in enumerate(tiles_of(NT)):
                for n in range(4):
                    w_ = wq[iw % 2]; iw += 1
                    for q in range(4):
                        P.dma("sp", w_[:, q * 4:(q + 1) * 4, :], wsrc[:, q * 4:(q + 1) * 4, n * 512:(n + 1) * 512])
                    p_ = pm[ip % 4]; ip += 1
                    for k in range(16):
                        P.mm(p_[:rows, :], h2T[:, k, t0:t0 + rows], w_[:, k, :], start=(k == 0), stop=(k == 15))
                    P.cp("act" if n % 2 == 0 else "dve", qs[:rows, n * 512:(n + 1) * 512], p_[:rows, :])
                for q in range(4):
                    pt = pst[ntr % 2]; ntr += 1
                    for kk in range(4):
                        k = q * 4 + kk
                        P.tr(pt[:, kk * 128:kk * 128 + rows], qs[:rows, k * 128:(k + 1) * 128], C["ident"][:rows, :rows])
                    P.cp("act" if q % 2 == 0 else "dve", qT[:, q * 4:(q + 1) * 4, :rows],
                         pt.v.r("p (a b) -> p a b", a=4)[:, :, :rows])
                for n in range(4):
                    p_ = pm[ip % 4]; ip += 1
                    for kk in range(4):
                        hj = n * 4 + kk
                        P.mm(p_[:rows, kk * 128:(kk + 1) * 128], qT[:, hj, :rows], skT[:, hj % 2, :])
                    P.cp("act" if n % 2 == 0 else "dve", sc[:rows, n * 512:(n + 1) * 512], p_[:rows, :])
                P.dma("sp", S["scores"][t0:t0 + rows, :], sc[:rows, :])
        self.tap("scores", S["scores"])

    def emit_t2_topk(self, l):
        P, I, S, C = self.P, self.I, self.S, self.C
        NEG = -1.0e30
        with P.scope():
            scb = [P.sb([128, 16, 128], name="sc%d" % i) for i in range(2)]
            scr = P.sb([128, 128], name="scr"); top = P.sb([128, 16, 16], name="top")
            cand = P.sb([128, 8, 256], name="cand"); scr2 = P.sb([128, 256], name="scr2")
            c16 = P.sb([128, 8, 16], name="c16"); ex16 = P.sb([128, 8, 16], name="ex16")
            negm = P.sb([128, 8], name="negm"); Z = P.sb([128, 8], name="Z"); nb = P.sb([128, 8], name="nb")
            tau = P.sb([128, 8], name="tau")
            Sq = [P.sb([128, 16, 128], name="Sq%d" % i) for i in range(2)]
            Ex = [P.sb([128, 16, 128], name="Ex%d" % i) for i in range(2)]
            gacc = [P.sb([128, 16, 128], name="gacc%d" % i) for i in range(2)]
            gts = [P.sb([128, 16, 128], name="gts%d" % i) for i in range(2)]
            pst = [P.ps([128, 512], name="pst%d" % i) for i in range(4)]
            ish = 0; ig = 0; ntr = 0
            for ti, (t0, rows) in enumerate(tiles_of(NT)):
                sc = scb[ti % 2]
                P.dma("sp", sc[:rows].r("p a b -> p (a b)"), S["scores"][t0:t0 + rows, :])
                R = slice(0, rows)
                for hj in range(16):
                    P.op("dve", lambda E: E.max(out=top[R, hj, 0:8].ap, in_=sc[R, hj, :].ap), reads=[sc.v], writes=[top.v])
                    P.op("dve", lambda E: E.match_replace(out=scr[R, :].ap, in_to_replace=top[R, hj, 0:8].ap,
                                                          in_values=sc[R, hj, :].ap, imm_value=NEG),
                         reads=[sc.v, top.v], writes=[scr.v])
                    P.op("dve", lambda E: E.max(out=top[R, hj, 8:16].ap, in_=scr[R, :].ap), reads=[scr.v], writes=[top.v])
                tv = top.v.r("p (h j) a -> p h j a", j=2)
                P.tt("dve", cand[R].r("p h (a b) -> p h a b", a=16),
                     tv[R, :, 0, :].unsq(3).bc([rows, 8, 16, 16]), tv[R, :, 1, :].unsq(2).bc([rows, 8, 16, 16]), ALU.add)
                for h in range(8):
                    P.op("dve", lambda E: E.max(out=c16[R, h, 0:8].ap, in_=cand[R, h, :].ap), reads=[cand.v], writes=[c16.v])
                    P.op("dve", lambda E: E.match_replace(out=scr2[R, :].ap, in_to_replace=c16[R, h, 0:8].ap,
                                                          in_values=cand[R, h, :].ap, imm_value=NEG),
                         reads=[cand.v, c16.v], writes=[scr2.v])
                    P.op("dve", lambda E: E.max(out=c16[R, h, 8:16].ap, in_=scr2[R, :].ap), reads=[scr2.v], writes=[c16.v])
                P.ts("dve", negm[R], c16[R, :, 0], -1.0, ALU.mult)
                P.cp("dve", tau[R], c16[R, :, 15])
                for h in range(8):
                    P.act(ex16[R, h, :], c16[R, h, :], AF.Exp, bias=negm[R, h:h + 1], accum=Z[R, h:h + 1])
                P.act(Z[R], Z[R], AF.Ln)
                P.tt("dve", nb[R], negm[R], Z[R], ALU.subtract)
                for iq in range(8):
                    ga = gacc[ig % 2]; ig += 1
                    for h in range(8):
                        s_ = Sq[ish % 2]; e_ = Ex[ish % 2]; ish += 1
                        P.tt("dve", s_[R], sc[R, 2 * h, iq * 16:(iq + 1) * 16].unsq(2).bc([rows, 16, 128]),
                             sc[R, 2 * h + 1, :].unsq(1).bc([rows, 16, 128]), ALU.add)
                        P.act(e_[R], s_[R], AF.Exp, bias=nb[R, h:h + 1])
                        if h == 0:
                            P.stt(ga[R], s_[R], tau[R, h:h + 1], e_[R], ALU.is_ge, ALU.mult)
                        else:
                            P.stt(e_[R], s_[R], tau[R, h:h + 1], e_[R], ALU.is_ge, ALU.mult)
                            P.tt("pool", ga[R], ga[R], e_[R], ALU.add)
                    g_ = gts[ig % 2]
                    for q in range(4):
                        pt = pst[ntr % 4]; ntr += 1
                        for kk in range(4):
                            i = q * 4 + kk
                            P.tr(pt[:, kk * 128:kk * 128 + rows], ga[R, i, :], C["ident"][:rows, :rows])
                        P.cp("act", g_[:, q * 4:(q + 1) * 4, :rows], pt.v.r("p (a b) -> p a b", a=4)[:, :, :rows])
                    P.dma("sp", S["GT"][iq * 2048:(iq + 1) * 2048, t0:t0 + rows].r("(i j) t -> j i t", j=128), g_[:, :, :rows])
        self.tap("GT", S["GT"])

    def emit_t2_pass1(self, l, W):
        P, I, S, C = self.P, self.I, self.S, self.C
        with P.scope():
            h2T = P.sb([128, 16, NT], BF16, name="h2T")
            for q in range(4):
                P.dma("sp", h2T[:, q * 4:(q + 1) * 4, :], S["h2Tb"].v.r("(k p) t -> p k t", p=128)[:, q * 4:(q + 1) * 4, :])
            ut = [P.sb([128, 16, 512], name="ut%d" % i) for i in range(2)]
            utb = [P.sb([128, 16, 512], BF16, name="utb%d" % i) for i in range(2)]
            ab = [P.sb([128, 512], BF16, name="ab%d" % i) for i in range(3)]
            a = [P.sb([128, 512], name="a%d" % i) for i in range(3)]
            g = [P.sb([128, 512], name="g%d" % i) for i in range(3)]
            pm = [P.ps([128, 512], name="pm%d" % i) for i in range(4)]
            usrc = W["ut"].v.r("(k p) n -> p k n", p=128)
            ip = 0; ia = 0
            for gi in range(32):
                uf = ut[gi % 2]; u_ = utb[gi % 2]
                for q in range(4):
                    P.dma("sp", uf[:, q * 4:(q + 1) * 4, :], usrc[:, q * 4:(q + 1) * 4, gi * 512:(gi + 1) * 512])
                P.cp("pool", u_[:, 0:8, :], uf[:, 0:8, :])
                P.cp("dve", u_[:, 8:12, :], uf[:, 8:12, :])
                P.cp("act", u_[:, 12:16, :], uf[:, 12:16, :])
                for c4 in range(4):
                    n0 = gi * 512 + c4 * 128
                    for (t0, tn) in TG:
                        p_ = pm[ip % 4]; ip += 1
                        a_ = a[ia % 3]; g_ = g[ia % 3]; ia += 1
                        P.dma("act", g_[:, :tn], S["GT"][n0:n0 + 128, t0:t0 + tn])
                        for k in range(16):
                            P.mm(p_[:, :tn], u_[:, k, c4 * 128:(c4 + 1) * 128], h2T[:, k, t0:t0 + tn], start=(k == 0), stop=(k == 15))
                        P.act(a_[:, :tn], p_[:, :tn], AF.Gelu_apprx_tanh)
                        ab_ = ab[ia % 3]
                        P.tt("dve" if ia % 2 else "pool", ab_[:, :tn], a_[:, :tn], g_[:, :tn], ALU.mult)
                        P.dma("sp", S["WT"][n0:n0 + 128, t0:t0 + tn], ab_[:, :tn])
        self.tap("WT", S["WT"])

    def emit_t2_pass2(self, l, W, last):
        P, I, S, C = self.P, self.I, self.S, self.C
        with P.scope():
            yacc = P.sb([128, 9, D], name="yacc")
            vg = [P.sb([128, 4, D], name="vg%d" % i) for i in range(2)]
            vgb = [P.sb([128, 4, D], BF16, name="vgb%d" % i) for i in range(2)]
            wt = [P.sb([128, 4, NT], BF16, name="wt%d" % i) for i in range(2)]
            pm = [P.ps([128, 512], name="pm%d" % i) for i in range(4)]
            ip = 0
            for gi in range(32):
                vf = vg[gi % 2]; v_ = vgb[gi % 2]; w_ = wt[gi % 2]
                P.dma("sp", vf.v, W["vv"][gi * 512:(gi + 1) * 512, :].r("(c p) d -> p c d", p=128))
                P.cp("pool", v_[:, 0:2, :], vf[:, 0:2, :])
                P.cp("act", v_[:, 2:4, :], vf[:, 2:4, :])
                P.dma("act", w_.v, S["WT"][gi * 512:(gi + 1) * 512, :].r("(c p) t -> p c t", p=128))
                for ti, (t0, rows) in enumerate(tiles_of(NT)):
                    for dg in range(4):
                        p_ = pm[ip % 4]; ip += 1
                        for c in range(4):
                            P.mm(p_[:rows, :], w_[:, c, t0:t0 + rows], v_[:, c, dg * 512:(dg + 1) * 512], start=(c == 0), stop=(c == 3))
                        dst = yacc[:rows, ti, dg * 512:(dg + 1) * 512]
                        if gi == 0:
                            P.cp("act", dst, p_[:rows, :])
                        else:
                            P.tt("dve", dst, dst, p_[:rows, :], ALU.add)
            M5 = P.sb([128, D], name="M5")
            xt = P.sb([128, D], name="xt")
            ss = P.sb([128, 1], name="ss")
            curk = -1
            if last:
                Gf = vg[0].v.r("p c d -> p (c d)")[:, 0:D]
                P.dma("act", Gf, I["gfin"].partition_broadcast(128))
                hh = vg[1].v.r("p c d -> p (c d)")[:, 0:D]
            for ti, (t0, rows) in enumerate(tiles_of(NT)):
                kind = kind_of_tile(t0)
                if kind != curk:
                    curk = kind
                    self.load_modvec(M5.v, l, kind, 5)
                P.dma("sp", xt[:rows, :], S["xres"][t0:t0 + rows, :])
                P.tt("dve", yacc[:rows, ti, :], yacc[:rows, ti, :], M5[:rows, :], ALU.mult)
                P.tt("dve", xt[:rows, :], xt[:rows, :], yacc[:rows, ti, :], ALU.add)
                P.dma("sp", S["xres"][t0:t0 + rows, :], xt[:rows, :])
                if last and t0 < 1024:
                    self.norm_rows(xt[:rows, :], rows, hh[:rows, :], ss[:rows, :], Gf[:rows, :], None)
                    P.dma("sp", self.out[t0:t0 + rows, :], hh[:rows, :])
        self.tap("xres_end", S["xres"])

    def build(self):
        P, I, S = self.P, self.I, self.S
        st = self.stages
        def on(name):
            return st is None or name in st
        if on("init"):
            for q in range(4):
                P.dma("pool", S["xres"][q * 272:(q + 1) * 272, :], I["x"][q * 272:(q + 1) * 272, :])
        if on("lmod"):
            self.emit_lmod()
        Wn = self.gather_weights(self.layers[0]) if on("gw") else None
        for idx, l in enumerate(self.layers):
            W = Wn
            if on("gw") and idx + 1 < len(self.layers):
                Wn = self.gather_weights(self.layers[idx + 1])
            if on("t1a"): self.emit_t1a(l)
            if on("t1b"): self.emit_t1b(l, W)
            if on("t1c"): self.emit_t1c(l)
            if on("rg"): self.emit_rg(l)
            if on("da"): self.emit_da(l)
            if on("hg"): self.emit_hg(l)
            if on("yx"):
                self.emit_yx()
            elif "y_loc" in self.dbg:
                self.tap("y_loc", S["y_loc"])
            if on("merge"): self.emit_t2_merge(l, W)
            if on("wout"): self.emit_t2_wout(l, W)
            if on("scores"): self.emit_t2_scores(l, W)
            if on("topk"): self.emit_t2_topk(l)
            if on("pass1"): self.emit_t2_pass1(l, W)
            if on("pass2"): self.emit_t2_pass2(l, W, l == 3)
        return P.finish()


OFF = {"rgx": 0, "rgg": 1024, "q": 2048, "k": 3072, "v": 4096, "hq": 5120, "zf": 6144, "zb": 7168, "hv": 8192, "hg": 9216}


def rope_tabs():
    S = 4096
    rows = S // 64
    r = np.repeat(np.arange(rows, dtype=np.float32), 64)
    col = np.tile(np.arange(64, dtype=np.float32), rows)
    inv = (np.float32(10000.0) ** (-np.arange(0, 32, 2, dtype=np.float32) / np.float32(32))).astype(np.float32)
    ang = np.concatenate([r[:, None] * inv, col[:, None] * inv], axis=-1).astype(np.float32)
    return np.cos(ang).astype(np.float32), np.sin(ang).astype(np.float32)


def hg_consts():
    s = np.arange(128)[:, None]
    t = np.arange(128)[None, :]
    same = (s // 64) == (t // 64)
    sameh = (s // 32) == (t // 32)
    out = []
    for d in range(2):
        if d == 0:
            LT = same & (s <= t)
            midw = (t // 32) * 32 + 15
            bnd = (t // 64) * 64 + 31
            Lmid = same & (s <= midw)
            Lb = same & (s <= bnd)
            Mw = sameh & (s <= t)
            Mc = same & ((s % 64) < 32) & ((t % 64) >= 32)
        else:
            LT = same & (s >= t)
            midw = (t // 32) * 32 + 16
            bnd = (t // 64) * 64 + 32
            Lmid = same & (s >= midw)
            Lb = same & (s >= bnd)
            Mw = sameh & (s >= t)
            Mc = same & ((s % 64) >= 32) & ((t % 64) < 32)
        f = lambda a: a.astype(np.float32)
        out += [f(LT), f(same) - f(LT), f(LT) - f(Lmid), f(LT) - f(Lb), f(Mw), f(Mc)]
    ci = np.zeros((128, 2), np.float32)
    ci[:64, 0] = 1
    ci[64:, 1] = 1
    return np.stack(out).astype(np.float32), ci


def wc_cols(c):
    cols = []
    for g in GRP:
        if g in ("qsw", "ksw"):
            base = OFF[g[0]] + c * 128
            idx = []
            for j in range(2):
                idx += list(range(base + j * 64 + 32, base + j * 64 + 64)) + list(range(base + j * 64, base + j * 64 + 32))
            cols += idx
        else:
            base = OFF[g] + c * 128
            cols += list(range(base, base + 128))
    return np.array(cols)


def pack_inputs(inp, layers=(0, 1, 2, 3), names=None):
    L = list(layers)
    cosT, sinT = rope_tabs()
    hgc, ci = hg_consts()
    pidx = np.arange(128) % 64
    cosF = np.ascontiguousarray(cosT[:, pidx % 32].T)
    sgn = np.where(pidx < 32, -1.0, 1.0).astype(np.float32)
    sinF = np.ascontiguousarray((sinT[:, pidx % 32] * sgn[None, :]).T)
    cv = np.stack([inp["c"][0], inp["c"][1], inp["c_ctx"]])
    cvT = np.ascontiguousarray(cv.T.reshape(16, 128, 3).transpose(1, 0, 2))
    outs = []

    def want(n):
        return names is None or n in names
    for r in range(8):
        d = {}
        if want("x"):
            d["x"] = np.concatenate([inp["x"][0, r * 512:(r + 1) * 512], inp["x"][1, r * 512:(r + 1) * 512],
                                     inp["ctx"][0, r * 32:(r + 1) * 32], inp["ctx"][1, r * 32:(r + 1) * 32]], axis=0)
        if want("cvT"): d["cvT"] = cvT
        if want("wada"): d["wada"] = np.ascontiguousarray(inp["w_ada"][L][:, :, r * 1536:(r + 1) * 1536])
        if want("bada"): d["bada"] = np.ascontiguousarray(inp["b_ada"][L][:, r * 1536:(r + 1) * 1536])
        if want("gmix"): d["gmix"] = np.ascontiguousarray(inp["norm_mix_g"][L])
        if want("gffn"): d["gffn"] = np.ascontiguousarray(inp["norm_ffn_g"][L])
        if want("gfin"): d["gfin"] = inp["final_norm_g"]
        if want("wc"):
            cols = wc_cols(r)
            d["wc"] = np.stack([inp["w_in"][l][:, cols] for l in L])
        if want("wg"): d["wg"] = np.stack([inp["w_in"][l][r * 256:(r + 1) * 256, 10240:] for l in L])
        if want("bgT"):
            d["bgT"] = np.ascontiguousarray(inp["b_gate"][L].reshape(len(L), 48, 128).transpose(2, 0, 1))
        sl = slice(r * 128, (r + 1) * 128)
        if want("rgcw"): d["rgcw"] = np.ascontiguousarray(inp["rg_conv_w"][L][:, :, sl].transpose(2, 0, 1))
        if want("rgcb"): d["rgcb"] = np.ascontiguousarray(inp["rg_conv_b"][L][:, sl].T)
        if want("rgwa"): d["rgwa"] = np.ascontiguousarray(inp["rg_wa"][L][:, :, r])
        if want("rgwx"): d["rgwx"] = np.ascontiguousarray(inp["rg_wx"][L][:, :, r])
        if want("rgba"): d["rgba"] = np.ascontiguousarray(inp["rg_ba"][L][:, :, sl].transpose(2, 0, 1))
        if want("rgbx"): d["rgbx"] = np.ascontiguousarray(inp["rg_bx"][L][:, :, sl].transpose(2, 0, 1))
        if want("rglam"): d["rglam"] = np.ascontiguousarray(inp["rg_lambda"][L][:, :, sl].transpose(2, 0, 1))
        if want("dalq"): d["dalq"] = np.ascontiguousarray(inp["da_lq"][L].reshape(len(L), 128))
        if want("dalk"): d["dalk"] = np.ascontiguousarray(inp["da_lk"][L].reshape(len(L), 128))
        if want("dasg"): d["dasg"] = np.ascontiguousarray(inp["da_subln_g"][L].T)
        if want("hglbF"): d["hglbF"] = np.ascontiguousarray(inp["hg_lb"][:, :, sl].transpose(2, 0, 1))
        if want("hglbT"): d["hglbT"] = np.ascontiguousarray(inp["hg_lb"][:, :, sl])
        if want("hgon"): d["hgon"] = np.ascontiguousarray(inp["hg_onorm_g"][L].T)
        if want("cosF"): d["cosF"] = cosF
        if want("sinF"): d["sinF"] = sinF
        if want("wbr"): d["wbr"] = np.stack([inp["w_branch"][l].reshape(3072, 2048)[r * 384:(r + 1) * 384] for l in L])
        if want("wout"): d["wout"] = np.stack([inp["w_out"][l][r * 256:(r + 1) * 256] for l in L])
        if want("wq"): d["wq"] = np.stack([inp["peer_wq"][l][r * 256:(r + 1) * 256] for l in L])
        if want("ut"): d["ut"] = np.stack([np.ascontiguousarray(inp["peer_u"][l][:, r * 256:(r + 1) * 256].T) for l in L])
        if want("vv"): d["vv"] = np.stack([inp["peer_v"][l][r * 2048:(r + 1) * 2048] for l in L])
        if want("skT"): d["skT"] = np.ascontiguousarray(inp["peer_subkeys"][L].transpose(0, 1, 3, 2))
        if want("ident"): d["ident"] = np.eye(128, dtype=np.float32)
        if want("hgc"): d["hgc"] = hgc
        if want("ci"): d["ci"] = ci
        outs.append({k: np.ascontiguousarray(v, dtype=np.float32) for k, v in d.items()})
    return outs


def unpack_output(res):
    out = np.zeros((2, 4096, 2048), np.float32)
    for r in range(8):
        o = res[r]["out"]
        out[0, r * 512:(r + 1) * 512] = o[:512]
        out[1, r * 512:(r + 1) * 512] = o[512:1024]
    return out


from concourse.bass_utils import run_bass_kernel_spmd

_CACHE = {}


def kernel(**inputs):
    inp = {k: np.asarray(v) for k, v in inputs.items()}
    if "nc" not in _CACHE:
        m = MK4(layers=(0, 1, 2, 3), stages=None, dbg=())
        _CACHE["nc"] = m.build()
        _CACHE["names"] = set(m.I.keys())
    ims = pack_inputs(inp, layers=(0, 1, 2, 3), names=_CACHE["names"])
    res = run_bass_kernel_spmd(_CACHE["nc"], ims, core_ids=list(range(8)))
    return unpack_output(res.results)
```
